# Optimizing a Trainium2 kernel written in Bass

```python
import math
import jax, jax.numpy as jnp
from jax import lax
import numpy as np

D_MODEL = 2048
BATCH = 4
SEQ = 8192
DEPTH = 1
DEC_BATCH = 32
DEC_SEQ = 16
PAST_LEN = 1024

CHUNK = 64
D_A = D_MODEL // 2
S5_GROUP = 16
G_A = D_A // S5_GROUP
N_STATE = 64
D_B = D_MODEL // 2
SC_WIDTH = 3
D_FF = 5504
FFN_WIDTH = 3
EPS = 1e-6
IN_COLS = D_A + 3 * D_B + 2 * D_MODEL

kernel_name = "hybrid_s5_shortconv_streaming_step"


def rmsnorm(x, g):
    xf = x.astype(jnp.float32)
    ms = jnp.mean(xf * xf, axis=-1, keepdims=True)
    return (xf * lax.rsqrt(ms + EPS) * g.astype(jnp.float32)).astype(x.dtype)


def causal_dwconv(x, buf, w, b):
    width = w.shape[0]
    L = x.shape[1]
    xp = jnp.concatenate([buf.astype(x.dtype), x], axis=1)
    y = xp[:, 0:L] * w[0]
    for k in range(1, width):
        y = y + xp[:, k:k + L] * w[k]
    return y + b, xp[:, -(width - 1):]


def s5_discretize(lam_re, lam_im, log_dt, b_re, b_im):
    f32 = jnp.float32
    dt = jnp.exp(log_dt.astype(f32))[:, None]
    lr, li = lam_re.astype(f32), lam_im.astype(f32)
    mag = jnp.exp(lr * dt)
    ab_re = mag * jnp.cos(li * dt)
    ab_im = mag * jnp.sin(li * dt)
    den = lr * lr + li * li
    nr, ni = ab_re - 1.0, ab_im
    f_re = (nr * lr + ni * li) / den
    f_im = (ni * lr - nr * li) / den
    br, bi = b_re.astype(f32), b_im.astype(f32)
    bb_re = f_re[..., None] * br - f_im[..., None] * bi
    bb_im = f_re[..., None] * bi + f_im[..., None] * br
    return ab_re, ab_im, bb_re, bb_im


def _complex_affine_combine(e1, e2):
    a1r, a1i, b1r, b1i = e1
    a2r, a2i, b2r, b2i = e2
    ar = a2r * a1r - a2i * a1i
    ai = a2r * a1i + a2i * a1r
    br = a2r * b1r - a2i * b1i + b2r
    bi = a2r * b1i + a2i * b1r + b2i
    return (ar, ai, br, bi)


def s5_block(u, h_re, h_im, ab_re, ab_im, bb_re, bb_im, c_re, c_im, d_skip):
    bu_re = jnp.einsum('blgp,gnp->blgn', u, bb_re)
    bu_im = jnp.einsum('blgp,gnp->blgn', u, bb_im)
    a_re = jnp.broadcast_to(ab_re, bu_re.shape)
    a_im = jnp.broadcast_to(ab_im, bu_re.shape)
    acr, aci, bcr, bci = lax.associative_scan(_complex_affine_combine, (a_re, a_im, bu_re, bu_im), axis=1)
    h0r, h0i = h_re[:, None], h_im[:, None]
    st_re = acr * h0r - aci * h0i + bcr
    st_im = acr * h0i + aci * h0r + bci
    y = (jnp.einsum('blgn,gpn->blgp', st_re, c_re) - jnp.einsum('blgn,gpn->blgp', st_im, c_im)
         + d_skip * u)
    return y, st_re[:, -1], st_im[:, -1]


def s5_mixer(u, h_re, h_im, disc, c_re, c_im, d_skip):
    f32 = jnp.float32
    ab_re, ab_im, bb_re, bb_im = disc
    cr, ci = c_re.astype(f32), c_im.astype(f32)
    dg = d_skip.astype(f32).reshape(G_A, S5_GROUP)
    bsz, L, _ = u.shape
    ug = u.astype(f32).reshape(bsz, L, G_A, S5_GROUP)
    if L > CHUNK:
        nc = L // CHUNK
        uc = ug.reshape(bsz, nc, CHUNK, G_A, S5_GROUP).transpose(1, 0, 2, 3, 4)

        def body(carry, u_blk):
            hr, hi = carry
            y_blk, hr, hi = s5_block(u_blk, hr, hi, ab_re, ab_im, bb_re, bb_im, cr, ci, dg)
            return (hr, hi), y_blk

        (h_re, h_im), ys = lax.scan(body, (h_re, h_im), uc)
        y = ys.transpose(1, 0, 2, 3, 4).reshape(bsz, L, D_A)
    else:
        y, h_re, h_im = s5_block(ug, h_re, h_im, ab_re, ab_im, bb_re, bb_im, cr, ci, dg)
        y = y.reshape(bsz, L, D_A)
    return y, h_re, h_im


def layer(x, s5_re, s5_im, sc_buf, ffn_buf, p):
    (norm1_g, w_in, lam_re, lam_im, log_dt, b_re, b_im, c_re, c_im, d_skip,
     w_glu, sc_conv_w, sc_conv_b, w_sc_out, w_o,
     norm2_g, w_up, ffn_conv_w, ffn_conv_b, w_down) = p
    f32 = jnp.float32
    dt = x.dtype
    h = rmsnorm(x, norm1_g)
    proj = h @ w_in
    u_a = proj[..., :D_A]
    g_in = proj[..., D_A:D_A + D_B]
    c_in = proj[..., D_A + D_B:D_A + 2 * D_B]
    v_in = proj[..., D_A + 2 * D_B:D_A + 3 * D_B]
    gates = jax.nn.sigmoid(proj[..., D_A + 3 * D_B:].astype(f32))
    g_a, g_b = gates[..., :D_MODEL], gates[..., D_MODEL:]
    disc = s5_discretize(lam_re, lam_im, log_dt, b_re, b_im)
    y_a, n_re, n_im = s5_mixer(u_a, s5_re.astype(f32), s5_im.astype(f32), disc, c_re, c_im, d_skip)
    y_a = jax.nn.gelu(y_a).astype(dt)
    glu = y_a @ w_glu
    br_a = glu[..., :D_MODEL].astype(f32) * jax.nn.sigmoid(glu[..., D_MODEL:].astype(f32))
    conv_out, new_sc = causal_dwconv(c_in * v_in, sc_buf, sc_conv_w, sc_conv_b)
    br_b = ((g_in * conv_out) @ w_sc_out).astype(f32)
    merged = (g_a * br_a + g_b * br_b).astype(dt)
    x = x + merged @ w_o
    h2 = rmsnorm(x, norm2_g)
    up = h2 @ w_up
    up_c, new_ffn = causal_dwconv(up, ffn_buf, ffn_conv_w, ffn_conv_b)
    val, gt = up_c[..., :D_FF], up_c[..., D_FF:]
    x = x + (jax.nn.silu(gt) * val) @ w_down
    return x, n_re, n_im, new_sc, new_ffn


def trunk(x, s5_re, s5_im, sc_buf, ffn_buf, layer_params, final_norm_g):
    new_re, new_im, new_sc, new_ffn = [], [], [], []
    for l in range(DEPTH):
        p = tuple(w[l] for w in layer_params)
        x, r, i, sc, ff = layer(x, s5_re[l], s5_im[l], sc_buf[l], ffn_buf[l], p)
        new_re.append(r.astype(s5_re.dtype))
        new_im.append(i.astype(s5_im.dtype))
        new_sc.append(sc.astype(sc_buf.dtype))
        new_ffn.append(ff.astype(ffn_buf.dtype))
    y = rmsnorm(x, final_norm_g)
    return y, jnp.stack(new_re), jnp.stack(new_im), jnp.stack(new_sc), jnp.stack(new_ffn)


def setup_inputs(seed: int = 0) -> dict:
    key = jax.random.key(seed)
    ks = jax.random.split(key, 32)
    f32 = jnp.float32

    def nrm(k, shape, s):
        return jax.random.normal(k, shape, f32) * s

    lam_im_base = jnp.pi * jnp.arange(N_STATE, dtype=f32)
    inp = {
        "x_prompt": nrm(ks[0], (BATCH, SEQ, D_MODEL), 1.0),
        "x_sample": nrm(ks[1], (DEC_BATCH, DEC_SEQ, D_MODEL), 1.0),
        "state_s5_re": nrm(ks[2], (DEPTH, DEC_BATCH, G_A, N_STATE), 0.5),
        "state_s5_im": nrm(ks[3], (DEPTH, DEC_BATCH, G_A, N_STATE), 0.5),
        "cache_sc_conv": nrm(ks[4], (DEPTH, DEC_BATCH, SC_WIDTH - 1, D_B), 0.5),
        "cache_ffn_conv": nrm(ks[5], (DEPTH, DEC_BATCH, FFN_WIDTH - 1, 2 * D_FF), 1.0),
        "norm1_g": 1.0 + nrm(ks[6], (DEPTH, D_MODEL), 0.02),
        "w_in": nrm(ks[7], (DEPTH, D_MODEL, IN_COLS), D_MODEL ** -0.5),
        "lam_re": -0.5 + nrm(ks[8], (DEPTH, G_A, N_STATE), 0.01),
        "lam_im": lam_im_base + nrm(ks[9], (DEPTH, G_A, N_STATE), 0.01),
        "log_dt": jax.random.uniform(ks[10], (DEPTH, G_A), f32, math.log(1e-3), math.log(1e-1)),
        "b_re": nrm(ks[11], (DEPTH, G_A, N_STATE, S5_GROUP), (2 * S5_GROUP) ** -0.5),
        "b_im": nrm(ks[12], (DEPTH, G_A, N_STATE, S5_GROUP), (2 * S5_GROUP) ** -0.5),
        "c_re": nrm(ks[13], (DEPTH, G_A, S5_GROUP, N_STATE), N_STATE ** -0.5),
        "c_im": nrm(ks[14], (DEPTH, G_A, S5_GROUP, N_STATE), N_STATE ** -0.5),
        "d_skip": nrm(ks[15], (DEPTH, D_A), 0.5),
        "w_glu": nrm(ks[16], (DEPTH, D_A, 2 * D_MODEL), D_A ** -0.5),
        "sc_conv_w": nrm(ks[17], (DEPTH, SC_WIDTH, D_B), SC_WIDTH ** -0.5),
        "sc_conv_b": nrm(ks[18], (DEPTH, D_B), 0.02),
        "w_sc_out": nrm(ks[19], (DEPTH, D_B, D_MODEL), D_B ** -0.5),
        "w_o": nrm(ks[20], (DEPTH, D_MODEL, D_MODEL), D_MODEL ** -0.5),
        "norm2_g": 1.0 + nrm(ks[21], (DEPTH, D_MODEL), 0.02),
        "w_up": nrm(ks[22], (DEPTH, D_MODEL, 2 * D_FF), D_MODEL ** -0.5),
        "ffn_conv_w": nrm(ks[23], (DEPTH, FFN_WIDTH, 2 * D_FF), FFN_WIDTH ** -0.5),
        "ffn_conv_b": nrm(ks[24], (DEPTH, 2 * D_FF), 0.02),
        "w_down": nrm(ks[25], (DEPTH, D_FF, D_MODEL), D_FF ** -0.5),
        "final_norm_g": 1.0 + nrm(ks[26], (D_MODEL,), 0.02),
    }
    return inp


def reference(x_prompt, x_sample, state_s5_re, state_s5_im, cache_sc_conv, cache_ffn_conv,
              norm1_g, w_in, lam_re, lam_im, log_dt, b_re, b_im, c_re, c_im, d_skip,
              w_glu, sc_conv_w, sc_conv_b, w_sc_out, w_o,
              norm2_g, w_up, ffn_conv_w, ffn_conv_b, w_down, final_norm_g):
    layer_params = (norm1_g, w_in, lam_re, lam_im, log_dt, b_re, b_im, c_re, c_im, d_skip,
                    w_glu, sc_conv_w, sc_conv_b, w_sc_out, w_o,
                    norm2_g, w_up, ffn_conv_w, ffn_conv_b, w_down)
    bp = x_prompt.shape[0]
    z_re = jnp.zeros((DEPTH, bp, G_A, N_STATE), state_s5_re.dtype)
    z_im = jnp.zeros((DEPTH, bp, G_A, N_STATE), state_s5_im.dtype)
    z_sc = jnp.zeros((DEPTH, bp, SC_WIDTH - 1, D_B), cache_sc_conv.dtype)
    z_ffn = jnp.zeros((DEPTH, bp, FFN_WIDTH - 1, 2 * D_FF), cache_ffn_conv.dtype)
    y_prompt, p_s5_re, p_s5_im, p_sc_conv, p_ffn_conv = trunk(
        x_prompt, z_re, z_im, z_sc, z_ffn, layer_params, final_norm_g)
    y_sample, s_s5_re, s_s5_im, s_sc_conv, s_ffn_conv = trunk(
        x_sample, state_s5_re, state_s5_im, cache_sc_conv, cache_ffn_conv, layer_params, final_norm_g)
    return (y_prompt, y_sample, p_s5_re, p_s5_im, p_sc_conv, p_ffn_conv,
            s_s5_re, s_s5_im, s_sc_conv, s_ffn_conv)
```

```python
import numpy as np
from contextlib import ExitStack
import concourse.bass as bass
import concourse.mybir as mybir
from concourse.bass_utils import run_bass_kernel_spmd

F32 = mybir.dt.float32
BF16 = mybir.dt.bfloat16
ALU = mybir.AluOpType
AF = mybir.ActivationFunctionType

D = 2048
KC = 16
NCC = 8
NFF = 43
NUP = 86
NPAIR = 32
PI = float(np.pi)
PI_SAFE = 3.1415925
C1 = float(np.float32(2 * np.pi))
C2 = float(2 * np.pi - np.float64(np.float32(2 * np.pi)))
EPS = 1e-6
ENGS = ("pe", "act", "dve", "pool", "sp")


class Sched:
    def __init__(self, nc, es):
        self.nc = nc
        self.es = es
        self.sems = {}
        self.cnt = {}
        self.e = {}
        for n in ENGS:
            self._sem("prog_" + n)
            self.e[n] = dict(sem="prog_" + n, waited={}, ops=[], pr=[], pw=[])
        self.tiles = {}
        self.nops = 0

    def _sem(self, name):
        if name not in self.sems:
            self.sems[name] = self.es.enter_context(self.nc.semaphore(name))
            self.cnt[name] = 0
        return name

    def op(self, eng, fn, reads=(), writes=(), ms=True, chan=None):
        e = self.e[eng]
        need = {}

        def add(d):
            for s, v in d.items():
                if need.get(s, 0) < v:
                    need[s] = v

        for k in reads:
            t = self.tiles.get(k)
            if t:
                add(t[0])
        for k in writes:
            t = self.tiles.get(k)
            if t:
                add(t[0])
                add(t[1])
        waits = []
        for s, v in need.items():
            if eng == "pe" and s == e["sem"]:
                continue
            if e["waited"].get(s, 0) < v:
                e["waited"][s] = v
                waits.append((self.sems[s], v))
        self.nops += 1
        if chan is not None:
            s = self._sem("ch_" + chan)
            self.cnt[s] += 16
            stamp = (s, self.cnt[s])
            inc = (self.sems[s], 16)
        elif ms:
            s = e["sem"]
            self.cnt[s] += 1
            stamp = (s, self.cnt[s])
            inc = (self.sems[s], 1)
        else:
            stamp = None
            inc = None

        def run(engine, waits=waits, fn=fn, inc=inc):
            for s, v in waits:
                engine.wait_ge(s, v)
            ins = fn(engine)
            if inc is not None:
                ins.then_inc(inc[0], inc[1])

        e["ops"].append(run)
        if stamp is None:
            e["pr"] += list(reads)
            e["pw"] += list(writes)
        else:
            allr = list(reads) + e["pr"]
            allw = list(writes) + e["pw"]
            e["pr"] = []
            e["pw"] = []
            for k in allw:
                self.tiles[k] = [{stamp[0]: stamp[1]}, {}]
            for k in allr:
                t = self.tiles.setdefault(k, [{}, {}])
                if t[1].get(stamp[0], 0) < stamp[1]:
                    t[1][stamp[0]] = stamp[1]
        return stamp

    def drain_key(self, eng, key):
        e = self.e[eng]
        t = self.tiles.get(key)
        waits = []
        if t:
            for d in t:
                for sname, v in d.items():
                    waits.append((self.sems[sname], v))

        def run(engine, waits=waits):
            for s_, v in waits:
                engine.wait_ge(s_, v)

        e["ops"].append(run)

    def final_wait(self, eng):
        e = self.e[eng]
        waits = [(self.sems[s], c) for s, c in self.cnt.items() if c > 0 and s != e["sem"]]

        def run(engine, waits=waits):
            for s, v in waits:
                engine.wait_ge(s, v)

        e["ops"].append(run)

    def emit(self):
        nc = self.nc
        with nc.Block() as block:
            @block.tensor
            def _(eng):
                for f in self.e["pe"]["ops"]:
                    f(eng)

            @block.scalar
            def _(eng):
                for f in self.e["act"]["ops"]:
                    f(eng)

            @block.vector
            def _(eng):
                for f in self.e["dve"]["ops"]:
                    f(eng)

            @block.gpsimd
            def _(eng):
                for f in self.e["pool"]["ops"]:
                    f(eng)

            @block.sync
            def _(eng):
                for f in self.e["sp"]["ops"]:
                    f(eng)
        for n in ENGS:
            self.e[n]["ops"] = []


def build(HALF):
    NBLK = HALF // 512
    PRE = HALF - 16
    nc = bass.Bass("TRN2", target_bir_lowering=False)

    def din(name, shape):
        return nc.dram_tensor(name, list(shape), F32, kind="ExternalInput").ap()

    def dout(name, shape):
        return nc.dram_tensor(name, list(shape), F32, kind="ExternalOutput").ap()

    xprev = din("xprev", [HALF, D])
    xsmall = din("xsmall", [80, D])
    xown = din("xown", [HALF, D])
    w_in = din("w_in", [D, 8192])
    w_glu = din("w_glu", [1024, 4096])
    w_sc = din("w_sc", [1024, 2048])
    w_o = din("w_o", [2048, 2048])
    w_up = din("w_up", [2048, 11008])
    w_dn = din("w_dn", [5504, 2048])
    bpad = din("bpad", [2, 32, 128, 128])
    cpad = din("cpad", [2, 32, 128, 128])
    lamre_d = din("lamre", [128, 32])
    lamim_d = din("lamim", [128, 32])
    logdt_d = din("logdt", [128, 32])
    sre_d = din("sre", [128, 4, 32])
    sim_d = din("sim", [128, 4, 32])
    dskip_d = din("dskip", [128, 8])
    scw_d = din("scw", [128, 8, 3])
    scb_d = din("scb", [128, 8])
    ffw_d = din("ffw", [128, 86, 3])
    ffb_d = din("ffb", [128, 86])
    n1g_d = din("n1g", [128, 16])
    n2g_d = din("n2g", [128, 16])
    fg_d = din("fg", [128, D])
    csc_d = din("csc", [128, 8, 4, 2])
    cff_d = din("cff", [128, 86, 4, 2])

    y_own = dout("y_own", [HALF, D])
    y_small = dout("y_small", [80, D])
    s5o_d = dout("s5o", [128, 2, 6, 32])
    sco_d = dout("sco", [128, 8, 6, 2])
    ffo_d = dout("ffo", [128, 86, 6, 2])

    win_b = nc.dram_tensor("win_b", [D, 8192], BF16).ap()
    wglu_b = nc.dram_tensor("wglu_b", [1024, 4096], BF16).ap()
    wsc_b = nc.dram_tensor("wsc_b", [1024, 2048], BF16).ap()
    wo_b = nc.dram_tensor("wo_b", [2048, 2048], BF16).ap()
    wup_b = nc.dram_tensor("wup_b", [2048, 11008], BF16).ap()
    wdn_b = nc.dram_tensor("wdn_b", [5504, 2048], BF16).ap()
    bc_b = nc.dram_tensor("bc_b", [8, 128, 4, 4, 128], BF16).ap()
    te_d = nc.dram_tensor("te_d", [32, 128, 2, 512], F32).ap()
    tw_d = nc.dram_tensor("tw_d", [32, 128, 2, 512], F32).ap()

    with ExitStack() as es:
        S = Sched(nc, es)

        def sb(name, shape, dt, stack=es):
            return stack.enter_context(nc.sbuf_tensor("s_" + name, list(shape), dt))

        ident = sb("ident", [128, 128], BF16)
        n1g = sb("n1g", [128, 16], F32)
        n2g = sb("n2g", [128, 16], F32)
        fg = sb("fg", [128, D], F32)
        r_t = sb("r_t", [128, 32], F32)
        fr_t = sb("fr_t", [128, 32], F32)
        fi_t = sb("fi_t", [128, 32], F32)
        dsk = sb("dsk", [128, 8], F32)
        scw = sb("scw", [128, 8, 3], F32)
        scb = sb("scb", [128, 8], F32)
        ffw = sb("ffw", [128, 86, 3], F32)
        ffb = sb("ffb", [128, 86], F32)
        Hr = sb("Hr", [128, 32], F32)
        Hi = sb("Hi", [128, 32], F32)
        SIr = sb("SIr", [128, 5, 32], F32)
        SIi = sb("SIi", [128, 5, 32], F32)
        SOr = sb("SOr", [128, 5, 32], F32)
        SOi = sb("SOi", [128, 5, 32], F32)
        s5st = sb("s5st", [128, 2, 6, 32], F32)
        hs_sc = sb("hs_sc", [128, 8, 5, 2], F32)
        hs_ff = sb("hs_ff", [128, 86, 5, 2], F32)
        tiny = sb("tiny", [128, 4, 16], F32)
        rL = sb("rL", [128, 32], F32)
        clT = sb("clT", [128, 32], F32)
        slT = sb("slT", [128, 32], F32)
        accp = sb("accp", [128, 32, 4], F32)
        gtr = sb("gtr", [128, 32], F32)
        gti = sb("gti", [128, 32], F32)
        gtt = sb("gtt", [128, 32], F32)
        ss = sb("ss", [128, 4], F32)
        rstd = sb("rstd", [128, 4], F32)

        with ExitStack() as ses:
            WK = {}

            def cast_w(src, dst, R, C, name, c_lo=0, c_hi=None):
                c_hi = C if c_hi is None else c_hi
                W = c_hi - c_lo
                a = 1
                while W // a > 2048 or W % a:
                    a += 1
                s3 = src[:, c_lo:c_hi].rearrange("r (a c) -> r a c", a=a)
                d3 = dst[:, c_lo:c_hi].rearrange("r (a c) -> r a c", a=a)
                step = max(1, 2048 // a)
                i = 0
                keys = []
                for r0 in range(0, R, step):
                    r1 = min(R, r0 + step)
                    S.op("pool", lambda e, r0=r0, r1=r1: e.dma_start(out=d3[r0:r1], in_=s3[r0:r1]),
                         writes=["%s_%d" % (name, i)], chan=name + str(i % 4))
                    keys.append("%s_%d" % (name, i))
                    i += 1
                WK[name] = keys

            cast_w(w_in, win_b, 2048, 8192, "win_u", 0, 1024)
            for w in range(2):
                for cc in range(8):
                    S.op("pool", lambda e, w=w, cc=cc: e.dma_start(
                        out=bc_b[cc, :, :, w, :], in_=bpad[w, 4 * cc:4 * cc + 4].rearrange("q p n -> p q n")),
                        writes=["bc_b_%d_%d" % (w, cc)], chan="bcb%d" % w)
            WK["bc_b"] = ["bc_b_%d_%d" % (w, cc) for w in range(2) for cc in range(8)] + ["bc_b"]
            cast_w(w_in, win_b, 2048, 8192, "win_b", 1024, 8192)

            def cload(dst_ap, src_ap, key):
                S.op("sp", lambda e: e.dma_start(out=dst_ap, in_=src_ap), writes=[key], chan="c_" + key)

            lamre = sb("lamre", [128, 32], F32, ses)
            lamim = sb("lamim", [128, 32], F32, ses)
            logdt = sb("logdt", [128, 32], F32, ses)
            sre = sb("sre", [128, 4, 32], F32, ses)
            sim = sb("sim", [128, 4, 32], F32, ses)
            cload(lamre[:], lamre_d, "lamre")
            cload(lamim[:], lamim_d, "lamim")
            cload(logdt[:], logdt_d, "logdt")
            cload(sre[:], sre_d, "sre")
            cload(sim[:], sim_d, "sim")
            cload(dsk[:], dskip_d, "dsk")
            cload(scw[:], scw_d, "scw")
            cload(scb[:], scb_d, "scb")
            cload(ffw[:], ffw_d, "ffw")
            cload(ffb[:], ffb_d, "ffb")
            cload(n1g[:], n1g_d, "n1g")
            cload(n2g[:], n2g_d, "n2g")
            cload(fg[:], fg_d, "fg")
            cload(hs_sc[:, :, 1:5, :], csc_d, "hs_sc")
            cload(hs_ff[:, :, 1:5, :], cff_d, "hs_ff")

            identf = sb("identf", [128, 128], F32, ses)
            S.op("pool", lambda e: e.memset(identf[:], 0.0), writes=["identf"])
            S.op("pool", lambda e: e.affine_select(out=identf[:], in_=identf[:], pattern=[[-1, 128]],
                                                      compare_op=ALU.not_equal, fill=1.0, base=0,
                                                      channel_multiplier=1),
                 reads=["identf"], writes=["identf"])
            S.op("dve", lambda e: e.tensor_copy(out=ident[:], in_=identf[:]), reads=["identf"], writes=["ident"])
            S.op("dve", lambda e: e.memset(Hr[:], 0.0), writes=["Hr"])
            S.op("dve", lambda e: e.memset(Hi[:], 0.0), writes=["Hi"])
            S.op("dve", lambda e: e.memset(hs_sc[:, :, 0, :], 0.0), writes=["hs_sc"])
            S.op("dve", lambda e: e.memset(hs_ff[:, :, 0, :], 0.0), writes=["hs_ff"])

            def st(name):
                return sb(name, [128, 32], F32, ses)

            dt_t, th, lrdt, c1, s1, ar, ai, kk, red, t1, t2, den, finr, fini = [st(n) for n in (
                "dt_t", "th", "lrdt", "c1", "s1", "ar", "ai", "kk", "red", "t1", "t2", "den", "finr", "fini")]

            def dve(fn, reads, writes):
                S.op("dve", fn, reads=reads, writes=writes)

            def actop(fn, reads, writes):
                S.op("act", fn, reads=reads, writes=writes)

            def tt(out, a, b, op, ko, ka, kb):
                dve(lambda e: e.tensor_tensor(out=out, in0=a, in1=b, op=op), [ka, kb], [ko])

            actop(lambda e: e.activation(out=dt_t[:], in_=logdt[:], func=AF.Exp), ["logdt"], ["dt_t"])
            tt(th[:], lamim[:], dt_t[:], ALU.mult, "th", "lamim", "dt_t")
            tt(lrdt[:], lamre[:], dt_t[:], ALU.mult, "lrdt", "lamre", "dt_t")
            actop(lambda e: e.activation(out=r_t[:], in_=lrdt[:], func=AF.Exp), ["lrdt"], ["r_t"])

            def sin_of(out, shift, ko):
                dve(lambda e: e.tensor_scalar(out=red[:], in0=th[:], scalar1=shift, scalar2=None, op0=ALU.add),
                    ["th"], ["red"])
                dve(lambda e: e.tensor_scalar(out=kk[:], in0=red[:], scalar1=PI, scalar2=None, op0=ALU.is_gt),
                    ["red"], ["kk"])
                for m in (3, 5, 7, 9, 11):
                    dve(lambda e, m=m: e.scalar_tensor_tensor(out=kk[:], in0=red[:], scalar=m * PI, in1=kk[:],
                                                               op0=ALU.is_gt, op1=ALU.add), ["red", "kk"], ["kk"])
                dve(lambda e: e.scalar_tensor_tensor(out=red[:], in0=kk[:], scalar=-C1, in1=red[:],
                                                      op0=ALU.mult, op1=ALU.add), ["kk", "red"], ["red"])
                dve(lambda e: e.scalar_tensor_tensor(out=red[:], in0=kk[:], scalar=-C2, in1=red[:],
                                                      op0=ALU.mult, op1=ALU.add), ["kk", "red"], ["red"])
                dve(lambda e: e.tensor_scalar(out=red[:], in0=red[:], scalar1=-PI_SAFE, scalar2=PI_SAFE,
                                                op0=ALU.max, op1=ALU.min), ["red"], ["red"])
                actop(lambda e: e.activation(out=out, in_=red[:], func=AF.Sin), ["red"], [ko])

            sin_of(s1[:], 0.0, "s1")
            sin_of(c1[:], PI / 2, "c1")
            tt(ar[:], r_t[:], c1[:], ALU.mult, "ar", "r_t", "c1")
            tt(ai[:], r_t[:], s1[:], ALU.mult, "ai", "r_t", "s1")
            dve(lambda e: e.tensor_scalar(out=ar[:], in0=ar[:], scalar1=-1.0, scalar2=None, op0=ALU.add),
                ["ar"], ["ar"])
            tt(t1[:], lamre[:], lamre[:], ALU.mult, "t1", "lamre", "lamre")
            tt(t2[:], lamim[:], lamim[:], ALU.mult, "t2", "lamim", "lamim")
            tt(den[:], t1[:], t2[:], ALU.add, "den", "t1", "t2")
            dve(lambda e: e.reciprocal(out=den[:], in_=den[:]), ["den"], ["den"])
            tt(t1[:], ar[:], lamre[:], ALU.mult, "t1", "ar", "lamre")
            tt(t2[:], ai[:], lamim[:], ALU.mult, "t2", "ai", "lamim")
            tt(t1[:], t1[:], t2[:], ALU.add, "t1", "t1", "t2")
            tt(fr_t[:], t1[:], den[:], ALU.mult, "fr_t", "t1", "den")
            tt(t1[:], ai[:], lamre[:], ALU.mult, "t1", "ai", "lamre")
            tt(t2[:], ar[:], lamim[:], ALU.mult, "t2", "ar", "lamim")
            tt(t1[:], t1[:], t2[:], ALU.subtract, "t1", "t1", "t2")
            tt(fi_t[:], t1[:], den[:], ALU.mult, "fi_t", "t1", "den")
            tt(t1[:], fr_t[:], fr_t[:], ALU.mult, "t1", "fr_t", "fr_t")
            tt(t2[:], fi_t[:], fi_t[:], ALU.mult, "t2", "fi_t", "fi_t")
            tt(t1[:], t1[:], t2[:], ALU.add, "t1", "t1", "t2")
            dve(lambda e: e.reciprocal(out=t1[:], in_=t1[:]), ["t1"], ["t1"])
            tt(finr[:], fr_t[:], t1[:], ALU.mult, "finr", "fr_t", "t1")
            dve(lambda e: e.scalar_tensor_tensor(out=fini[:], in0=fi_t[:], scalar=-1.0, in1=t1[:],
                                                  op0=ALU.mult, op1=ALU.mult), ["fi_t", "t1"], ["fini"])
            sA = sb("sA", [128, 4, 32], F32, ses)
            sB = sb("sB", [128, 4, 32], F32, ses)
            bc4 = lambda t: t[:].unsqueeze(1).to_broadcast([128, 4, 32])
            tt(sA[:], sre[:], bc4(finr), ALU.mult, "sA", "sre", "finr")
            tt(sB[:], sim[:], bc4(fini), ALU.mult, "sB", "sim", "fini")
            tt(SIr[:, 1:5, :], sA[:], sB[:], ALU.subtract, "SIr", "sA", "sB")
            tt(sA[:], sre[:], bc4(fini), ALU.mult, "sA", "sre", "fini")
            tt(sB[:], sim[:], bc4(finr), ALU.mult, "sB", "sim", "finr")
            tt(SIi[:, 1:5, :], sA[:], sB[:], ALU.add, "SIi", "sA", "sB")

            with ExitStack() as cs:
                Cr = sb("Cr_s", [128, 32, 128], F32, cs)
                Ci = sb("Ci_s", [128, 32, 128], F32, cs)
                Ta = sb("Ta_s", [128, 32, 128], F32, cs)
                Tb = sb("Tb_s", [128, 32, 128], F32, cs)
                Cst = sb("Cst_s", [128, 32, 2, 128], BF16, cs)
                cload(Cr[:], cpad[0].rearrange("pr p n -> p pr n"), "Cr")
                cload(Ci[:], cpad[1].rearrange("pr p n -> p pr n"), "Ci")
                bcf = lambda t: t[:].unsqueeze(2).to_broadcast([128, 32, 128])
                tt(Ta[:], Cr[:], bcf(fr_t), ALU.mult, "Ta", "Cr", "fr_t")
                tt(Tb[:], Ci[:], bcf(fi_t), ALU.mult, "Tb", "Ci", "fi_t")
                tt(Cst[:, :, 0, :], Ta[:], Tb[:], ALU.subtract, "Cst", "Ta", "Tb")
                tt(Ta[:], Cr[:], bcf(fi_t), ALU.mult, "Ta", "Cr", "fi_t")
                tt(Tb[:], Ci[:], bcf(fr_t), ALU.mult, "Tb", "Ci", "fr_t")
                dve(lambda e: e.scalar_tensor_tensor(out=Cst[:, :, 1, :], in0=Ta[:], scalar=-1.0, in1=Tb[:],
                                                      op0=ALU.mult, op1=ALU.subtract), ["Ta", "Tb"], ["Cst"])
                for cc in range(8):
                    S.op("sp", lambda e, cc=cc: e.dma_start(
                        out=bc_b[cc, :, :, 2:4, :], in_=Cst[:, 4 * cc:4 * cc + 4, :, :]),
                        reads=["Cst"], writes=["bc_b"], chan="bcc")
                S.drain_key("sp", "Cst")
                S.emit()

            Ec = sb("Ec", [128, 16, 512], F32, ses)
            Es = sb("Es", [128, 16, 512], F32, ses)
            Xa = sb("Xa", [128, 16, 256], F32, ses)
            Xb = sb("Xb", [128, 16, 256], F32, ses)
            Rp = sb("Rp", [128, 16, 512], F32, ses)
            rpw = sb("rpw", [128, 16], F32, ses)
            for hf in range(2):
                p0 = hf * 16
                dve(lambda e, p0=p0: e.tensor_copy(out=Ec[:, :, 0], in_=c1[:, p0:p0 + 16]), ["c1"], ["Ec"])
                dve(lambda e, p0=p0: e.tensor_copy(out=Es[:, :, 0], in_=s1[:, p0:p0 + 16]), ["s1"], ["Es"])
                n = 1
                while n < 512:
                    cm = Ec[:, :, n - 1:n].to_broadcast([128, 16, n])
                    sm = Es[:, :, n - 1:n].to_broadcast([128, 16, n])
                    xa = Xa[:, :, 0:n]
                    xb = Xb[:, :, 0:n]
                    tt(xa, Ec[:, :, 0:n], cm, ALU.mult, "Xa", "Ec", "Ec")
                    tt(xb, Es[:, :, 0:n], sm, ALU.mult, "Xb", "Es", "Es")
                    tt(Ec[:, :, n:2 * n], xa, xb, ALU.subtract, "Ec", "Xa", "Xb")
                    tt(xa, Ec[:, :, 0:n], sm, ALU.mult, "Xa", "Ec", "Es")
                    tt(xb, Es[:, :, 0:n], cm, ALU.mult, "Xb", "Es", "Ec")
                    tt(Es[:, :, n:2 * n], xa, xb, ALU.add, "Es", "Xa", "Xb")
                    n *= 2
                S.op("sp", lambda e, p0=p0: e.dma_start(
                    out=te_d[p0:p0 + 16, :, 0, :].rearrange("pr p t -> p pr t"), in_=Ec[:]),
                    reads=["Ec"], writes=["te_d"], chan="te0")
                S.op("sp", lambda e, p0=p0: e.dma_start(
                    out=te_d[p0:p0 + 16, :, 1, :].rearrange("pr p t -> p pr t"), in_=Es[:]),
                    reads=["Es"], writes=["te_d"], chan="te1")
                dve(lambda e, p0=p0: e.tensor_copy(out=clT[:, p0:p0 + 16], in_=Ec[:, :, 511]), ["Ec"], ["clT"])
                dve(lambda e, p0=p0: e.tensor_copy(out=slT[:, p0:p0 + 16], in_=Es[:, :, 511]), ["Es"], ["slT"])
                dve(lambda e, p0=p0: e.tensor_copy(out=rpw[:], in_=r_t[:, p0:p0 + 16]), ["r_t"], ["rpw"])
                dve(lambda e: e.memset(Rp[:, :, 511:512], 1.0), [], ["Rp"])
                n = 1
                while n < 512:
                    tt(Rp[:, :, 512 - 2 * n:512 - n], Rp[:, :, 512 - n:512],
                       rpw[:].unsqueeze(2).to_broadcast([128, 16, n]), ALU.mult, "Rp", "Rp", "rpw")
                    tt(rpw[:], rpw[:], rpw[:], ALU.mult, "rpw", "rpw", "rpw")
                    n *= 2
                dve(lambda e, p0=p0: e.tensor_copy(out=rL[:, p0:p0 + 16], in_=rpw[:]), ["rpw"], ["rL"])
                tt(Ec[:], Ec[:], Rp[:], ALU.mult, "Ec", "Ec", "Rp")
                tt(Es[:], Es[:], Rp[:], ALU.mult, "Es", "Es", "Rp")
                S.op("sp", lambda e, p0=p0: e.dma_start(
                    out=tw_d[p0:p0 + 16, :, 0, :].rearrange("pr p t -> p pr t"), in_=Ec[:]),
                    reads=["Ec"], writes=["tw_d"], chan="te0")
                S.op("sp", lambda e, p0=p0: e.dma_start(
                    out=tw_d[p0:p0 + 16, :, 1, :].rearrange("pr p t -> p pr t"), in_=Es[:]),
                    reads=["Es"], writes=["tw_d"], chan="te1")
            S.drain_key("sp", "Ec")
            S.drain_key("sp", "Es")
            S.emit()
        cast_w(w_glu, wglu_b, 1024, 4096, "wglu_b")
        cast_w(w_sc, wsc_b, 1024, 2048, "wsc_b")
        cast_w(w_o, wo_b, 2048, 2048, "wo_b")

        xt = sb("xt", [128, 4, D], F32)
        hb = sb("hb", [128, 2, D], BF16)
        hT = sb("hT", [128, KC, 512], BF16)
        wring = sb("wring", [128, 5, 2048], BF16)
        tabs = sb("tabs", [128, 2, 2, 2, 512], F32)
        ystage = hb[:].rearrange("p a d -> p (a d)").bitcast(F32)
        U = sb("U", [128, 23584], F32)
        ps = es.enter_context(nc.psum_tensor("ps", [128, 8, 512], F32))

        def carve(off_f32, nelem, dt):
            if dt == F32:
                return U[:, off_f32:off_f32 + nelem], off_f32 + nelem
            return U[:, off_f32:off_f32 + nelem // 2].bitcast(BF16), off_f32 + nelem // 2

        o = 0
        u_v, o = carve(o, 8 * 512, BF16)
        scin_v, o = carve(o, 8 * 512, BF16)
        ya_v, o = carve(o, 8 * 512, BF16)
        mg_v, o = carve(o, 16 * 512, BF16)
        gin_v, o = carve(o, 4 * 512, F32)
        G_v, o = carve(o, 4 * 512, F32)
        tmp_v, o = carve(o, 8 * 512, F32)
        hbf_v, o = carve(o, 4 * 512, BF16)
        sig_v, o = carve(o, 6 * 512, BF16)
        xsc_v, o = carve(o, 520, F32)
        ctmp_v, o = carve(o, 2 * 512, F32)
        accs_v, o = carve(o, 2 * 512, F32)
        assert o <= 23584, o
        accs_v = accs_v.rearrange("p (c t) -> p c t", c=2)
        ctmp_v = ctmp_v.rearrange("p (c t) -> p c t", c=2)
        u_v = u_v.rearrange("p (c t) -> p c t", c=8)
        scin_v = scin_v.rearrange("p (c t) -> p c t", c=8)
        ya_v = ya_v.rearrange("p (c t) -> p c t", c=8)
        mg_v = mg_v.rearrange("p (c t) -> p c t", c=16)
        gin_v = gin_v.rearrange("p (c t) -> p c t", c=4)
        G_v = G_v.rearrange("p (c t) -> p c t", c=4)
        tmp_v = tmp_v.rearrange("p (c t) -> p c t", c=8)
        hbf_v = hbf_v.rearrange("p (c t) -> p c t", c=4)
        sig_v = sig_v.rearrange("p (c t) -> p c t", c=6)
        u_alt = U[:, 2048:4096].bitcast(BF16).rearrange("p (c t) -> p c t", c=8)
        hT_alt = U[:, 6144:10240].bitcast(BF16).rearrange("p (c t) -> p c t", c=16)
        o = 0
        act_v, o = carve(o, 43 * 512, BF16)
        xff_v, o = carve(o, 4 * 520, F32)
        acf_v, o = carve(o, 4 * 512, F32)
        assert o <= 23584, o
        act_v = act_v.rearrange("p (c t) -> p c t", c=43)
        xff_v = xff_v.rearrange("p (c t) -> p c t", c=4)
        acf_v = acf_v.rearrange("p (c t) -> p c t", c=4)

        cin = [U[:, 10240 + 2048 * k:10240 + 2048 * (k + 1)] for k in range(2)]
        cout = [U[:, 4096 + 1024 * k:4096 + 1024 * (k + 1)].bitcast(BF16) for k in range(2)]
        chunks = []
        for rc in range(16):
            for a_ in range(8):
                chunks.append(("wup_b", w_up, wup_b, 128 * rc, 1376 * a_, 1376))
        for rc in range(43):
            chunks.append(("wdn_b", w_dn, wdn_b, 128 * rc, 0, 2048))
        WK["wup_b"] = []
        WK["wdn_b"] = []

        def cload_(i):
            nm, src, dst, r0, c0, nc_ = chunks[i]
            k = i % 2
            S.op("pool", lambda e: e.dma_start(out=cin[k][:, 0:nc_], in_=src[r0:r0 + 128, c0:c0 + nc_]),
                 writes=["cin%d" % k], chan="cin%d" % k)

        cload_(0)
        for i, (nm, src, dst, r0, c0, nc_) in enumerate(chunks):
            k = i % 2
            if i + 1 < len(chunks):
                cload_(i + 1)
            S.op("pool", lambda e, k=k, nc_=nc_: e.tensor_copy(out=cout[k][:, 0:nc_], in_=cin[k][:, 0:nc_]),
                 reads=["cin%d" % k], writes=["cout%d" % k])
            pk = "%s_p%d" % (nm, i)
            S.op("pool", lambda e, k=k, nc_=nc_, dst=dst, r0=r0, c0=c0: e.dma_start(
                out=dst[r0:r0 + 128, c0:c0 + nc_], in_=cout[k][:, 0:nc_]),
                reads=["cout%d" % k], writes=[pk], chan="cout%d" % k)
            WK[nm].append(pk)

        st8 = dict(bank=0, ring=0, tab=0, hb=0, ff=0, b4=0, yb=6, s5mode=False, b45=0)

        def bank():
            if st8["s5mode"]:
                b = 4 + st8["b45"]
                st8["b45"] ^= 1
                return b
            b = st8["bank"]
            st8["bank"] = (b + 1) % 6
            return b

        def bank2():
            b = st8["bank"]
            if b % 2:
                b = (b + 1) % 6
            st8["bank"] = (b + 2) % 6
            return b

        def bank4():
            b = st8["b4"]
            st8["b4"] = 4 - b
            st8["bank"] = 0
            return b

        def wload(src3, a, b, key_src):
            slot = st8["ring"]
            st8["ring"] = (slot + 1) % 4
            dst = wring[:, slot, 0:a * b].rearrange("p (a b) -> p a b", a=a)
            S.op("sp", lambda e: e.dma_start(out=dst, in_=src3), reads=WK.get(key_src, [key_src]),
                 writes=["w%d" % slot], chan="w%d" % slot)
            return dst, "w%d" % slot

        def bcload(src3, a, b):
            dst = wring[:, 4, 0:a * b].rearrange("p (a b) -> p a b", a=a)
            S.op("sp", lambda e: e.dma_start(out=dst, in_=src3), reads=WK["bc_b"], writes=["w4"], chan="w4")
            return dst, "w4"

        def ftile(wb, key, r0, nk, c0, ncol):
            src = wb[r0:r0 + 128 * nk, c0:c0 + ncol].rearrange("(k p) c -> p k c", p=128)
            return wload(src, nk, ncol, key)

        def pe(fn, reads, writes, ms):
            S.op("pe", fn, reads=reads, writes=writes, ms=ms)

        def dve(fn, reads, writes):
            S.op("dve", fn, reads=reads, writes=writes)

        def actop(fn, reads, writes):
            S.op("act", fn, reads=reads, writes=writes)

        def psk(b):
            return "ps%d" % b

        def norm_T(tts, gainT, gkey, hTd, hTk):
            for ti, (c0, R) in enumerate(tts):
                hslot = st8["hb"]
                st8["hb"] = 1 - hslot
                hk = "hb%d" % hslot
                xk = "x%d" % ti
                actop(lambda e, ti=ti, R=R, hslot=hslot: e.activation(
                    out=hb[0:R, hslot, :], in_=xt[0:R, ti, :], func=AF.Square, accum_out=ss[0:R, ti:ti + 1]),
                    [xk], [hk, "ss%d" % ti])
                actop(lambda e, ti=ti, R=R: e.activation(
                    out=rstd[0:R, ti:ti + 1], in_=ss[0:R, ti:ti + 1], func=AF.Sqrt, scale=1.0 / D, bias=EPS),
                    ["ss%d" % ti], ["rstd%d" % ti])
                dve(lambda e, ti=ti, R=R: e.reciprocal(out=rstd[0:R, ti:ti + 1], in_=rstd[0:R, ti:ti + 1]),
                    ["rstd%d" % ti], ["rstd%d" % ti])
                actop(lambda e, ti=ti, R=R, hslot=hslot: e.activation(
                    out=hb[0:R, hslot, :], in_=xt[0:R, ti, :], func=AF.Copy, scale=rstd[0:R, ti:ti + 1]),
                    [xk, "rstd%d" % ti], [hk])
                b = bank2()
                pT = ps[:, b:b + 2, :].rearrange("p a b -> p (a b)").bitcast(BF16).rearrange(
                    "p (k t) -> p k t", k=16)
                for kc in range(KC):
                    pe(lambda e, kc=kc, R=R, hslot=hslot, pT=pT: e.transpose(
                        out=pT[:, kc, 0:R], in_=hb[0:R, hslot, kc * 128:(kc + 1) * 128], identity=ident[0:R, 0:R]),
                        [hk, "ident"], [psk(b), psk(b + 1)], ms=(kc == KC - 1))
                dve(lambda e, c0=c0, R=R, pT=pT: e.tensor_tensor(
                    out=hTd[:, :, c0:c0 + R], in0=pT[:, :, 0:R],
                    in1=gainT[:].unsqueeze(2).to_broadcast([128, 16, R]), op=ALU.mult),
                    [psk(b), psk(b + 1), gkey], [hTk])

        def proj_pair(c0col, n, hTs, hTk):
            b0, b1 = bank(), bank()
            bs = (b0, b1)
            for h in range(2):
                wt, wk = ftile(win_b, "win_u" if c0col < 1024 else "win_b", h * 1024, 8, c0col, 256)
                for ci in range(2):
                    for k in range(8):
                        kc = h * 8 + k
                        pe(lambda e, wt=wt, ci=ci, k=k, kc=kc: e.matmul(
                            ps[:, bs[ci], 0:n], lhsT=wt[:, k, ci * 128:(ci + 1) * 128], rhs=hTs[:, kc, 0:n],
                            start=(kc == 0), stop=(kc == KC - 1)),
                            [wk, hTk], [psk(bs[ci])], ms=(k == 7))
            return bs

        def s5_phase(n, nseg, L, SIr_, SIi_, SOr_, SOi_, sik, sok, state_only, uS, uK, extras=()):
            yb = None

            def tt(out, a, bb, op, ko, ka, kb, eng="dve"):
                S.op(eng, lambda e: e.tensor_tensor(out=out, in0=a, in1=bb, op=op), reads=ka + kb, writes=ko)

            def tload(src, pr):
                k = st8["tab"]
                st8["tab"] = (k + 1) % 4
                tk = "tab%d" % k
                S.op("sp", lambda e: e.dma_start(out=tabs[:, k // 2, k % 2, :, :], in_=src[pr]),
                     reads=["te_d" if src is te_d else "tw_d"], writes=[tk], chan=tk)
                return k, tk

            if state_only:
                for cc in range(NCC):
                    bct, bck = bcload(bc_b[cc, :, :, 0:2, :], 8, 128)
                    for q in range(4):
                        pr = 4 * cc + q
                        k, tk = tload(tw_d, pr)
                        b0, b1 = bank(), bank()
                        pe(lambda e, bct=bct, q=q, cc=cc, b0=b0: e.matmul(
                            ps[:, b0, 0:n], lhsT=bct[:, 2 * q, :], rhs=uS[:, cc, 0:n], start=True, stop=True),
                            [bck, uK % cc], [psk(b0)], ms=False)
                        pe(lambda e, bct=bct, q=q, cc=cc, b1=b1: e.matmul(
                            ps[:, b1, 0:n], lhsT=bct[:, 2 * q + 1, :], rhs=uS[:, cc, 0:n], start=True, stop=True),
                            [bck, uK % cc], [psk(b1)], ms=True)
                        for kk_, (bb_, w) in enumerate(((b0, 0), (b1, 1), (b1, 0), (b0, 1))):
                            dve(lambda e, kk_=kk_, bb_=bb_, w=w, pr=pr, k=k: e.scalar_tensor_tensor(
                                out=tmp_v[:, kk_ % 2, 0:n], in0=ps[:, bb_, 0:n], scalar=1.0,
                                in1=tabs[:, k // 2, k % 2, w, 0:n],
                                op0=ALU.mult, op1=ALU.mult, accum_out=accp[:, pr, kk_:kk_ + 1]),
                                [psk(bb_), tk], ["tmp%d" % (kk_ % 2), "accp"])
                A = lambda k: accp[:, :, k]
                tt(gtr[:], Hr[:], rL[:], ALU.mult, ["gtr"], ["H"], ["rL"])
                tt(gtr[:], gtr[:], A(0), ALU.add, ["gtr"], ["gtr"], ["accp"])
                tt(gtr[:], gtr[:], A(1), ALU.add, ["gtr"], ["gtr"], ["accp"])
                tt(gti[:], Hi[:], rL[:], ALU.mult, ["gti"], ["H"], ["rL"])
                tt(gti[:], gti[:], A(2), ALU.add, ["gti"], ["gti"], ["accp"])
                tt(gti[:], gti[:], A(3), ALU.subtract, ["gti"], ["gti"], ["accp"])
                tt(gtt[:], gti[:], slT[:], ALU.mult, ["gtt"], ["gti"], ["slT"])
                tt(Hr[:], gtr[:], clT[:], ALU.mult, ["H"], ["gtr"], ["clT"])
                tt(Hr[:], Hr[:], gtt[:], ALU.subtract, ["H"], ["H"], ["gtt"])
                tt(gtt[:], gti[:], clT[:], ALU.mult, ["gtt"], ["gti"], ["clT"])
                tt(Hi[:], gtr[:], slT[:], ALU.mult, ["H"], ["gtr"], ["slT"])
                tt(Hi[:], Hi[:], gtt[:], ALU.add, ["H"], ["H"], ["gtt"])
                return

            def v4(ap3):
                return ap3.rearrange("p j (s l) -> p j s l", s=nseg)

            for cc in range(NCC):
                bct, bck = bcload(bc_b[cc].rearrange("p q w n -> p (q w) n"), 16, 128)
                for pp in range(2):
                    pr0 = 4 * cc + 2 * pp
                    ks = [tload(te_d, pr0 + j) for j in range(2)]
                    assert ks[0][0] % 2 == 0
                    tb = ks[0][0] // 2
                    tks = [ks[0][1], ks[1][1]]
                    for j in range(2):
                        q = 2 * pp + j
                        pe(lambda e, bct=bct, q=q, cc=cc, j=j: e.matmul(
                            ps[:, j, 0:n], lhsT=bct[:, 4 * q + 0, :], rhs=uS[:, cc, 0:n], start=True, stop=True),
                            [bck, uK % cc], [psk(j)], ms=False)
                        pe(lambda e, bct=bct, q=q, cc=cc, j=j: e.matmul(
                            ps[:, 2 + j, 0:n], lhsT=bct[:, 4 * q + 1, :], rhs=uS[:, cc, 0:n], start=True, stop=True),
                            [bck, uK % cc], [psk(2 + j)], ms=(j == 1))
                    if nseg == 1:
                        bcT = lambda w: tabs[:, tb, :, w, 0:L]
                        vv = lambda ap3: ap3
                    else:
                        bcT = lambda w: tabs[:, tb, :, w, 0:L].unsqueeze(2).to_broadcast([128, 2, nseg, L])
                        vv = v4
                    cT, sT = bcT(0), bcT(1)
                    PrB = vv(ps[:, 0:2, 0:n])
                    PiB = vv(ps[:, 2:4, 0:n])
                    T = [vv(tmp_v[:, 2 * i:2 * i + 2, 0:n]) for i in range(4)]
                    TK = [["tmp%d" % i] for i in range(4)]
                    gr = vv(gin_v[:, 0:2, 0:n])
                    gi = vv(gin_v[:, 2:4, 0:n])
                    kPr = [psk(0), psk(1)]
                    kPi = [psk(2), psk(3)]
                    tt(T[0], PrB, cT, ALU.mult, TK[0], kPr, tks)
                    tt(T[1], PiB, sT, ALU.mult, TK[1], kPi, tks)
                    tt(gr, T[0], T[1], ALU.add, ["gin0"], TK[0], TK[1], eng="pool")
                    tt(T[2], PiB, cT, ALU.mult, TK[2], kPi, tks)
                    tt(T[3], PrB, sT, ALU.mult, TK[3], kPr, tks)
                    tt(gi, T[2], T[3], ALU.subtract, ["gin1"], TK[2], TK[3], eng="pool")
                    for w, (SI_, gk) in enumerate(((SIr_, "gin0"), (SIi_, "gin1"))):
                        for j in range(2):
                            for sg in range(nseg):
                                dve(lambda e, sg=sg, w=w, j=j, SI_=SI_, pr=pr0 + j: e.tensor_tensor_scan(
                                    out=G_v[:, 2 * w + j, sg * L:(sg + 1) * L],
                                    data0=r_t[:, pr:pr + 1].to_broadcast([128, L]),
                                    data1=gin_v[:, 2 * w + j, sg * L:(sg + 1) * L],
                                    initial=SI_[:, sg, pr:pr + 1], op0=ALU.mult, op1=ALU.add),
                                    [gk, "r_t", sik], ["G%d" % w])
                    GrL = v4(G_v[:, 0:2, 0:n])[:, :, :, L - 1]
                    GiL = v4(G_v[:, 2:4, 0:n])[:, :, :, L - 1]
                    clB = tabs[:, tb, :, 0, L - 1:L].to_broadcast([128, 2, nseg])
                    slB = tabs[:, tb, :, 1, L - 1:L].to_broadcast([128, 2, nseg])
                    tA = tiny[:, 0, 0:2 * nseg].rearrange("p (j s) -> p j s", j=2)
                    tB = tiny[:, 1, 0:2 * nseg].rearrange("p (j s) -> p j s", j=2)
                    sor = SOr_[:, :, pr0:pr0 + 2].rearrange("p s q -> p q s")
                    soi = SOi_[:, :, pr0:pr0 + 2].rearrange("p s q -> p q s")
                    tt(tA, GiL, slB, ALU.mult, ["tinyA"], ["G1"], tks)
                    tt(tB, GrL, clB, ALU.mult, ["tinyB"], ["G0"], tks)
                    tt(sor, tB, tA, ALU.subtract, [sok], ["tinyB"], ["tinyA"])
                    tt(tA, GiL, clB, ALU.mult, ["tinyA"], ["G1"], tks)
                    tt(tB, GrL, slB, ALU.mult, ["tinyB"], ["G0"], tks)
                    tt(soi, tB, tA, ALU.add, [sok], ["tinyB"], ["tinyA"])
                    GrB = vv(G_v[:, 0:2, 0:n])
                    GiB = vv(G_v[:, 2:4, 0:n])
                    hr = vv(hbf_v[:, 0:2, 0:n])
                    hi = vv(hbf_v[:, 2:4, 0:n])
                    tt(T[0], GrB, cT, ALU.mult, TK[0], ["G0"], tks)
                    tt(T[1], GiB, sT, ALU.mult, TK[1], ["G1"], tks)
                    tt(hr, T[0], T[1], ALU.subtract, ["hbfr"], TK[0], TK[1], eng="pool")
                    tt(T[2], GrB, sT, ALU.mult, TK[2], ["G0"], tks, eng="pool")
                    tt(T[3], GiB, cT, ALU.mult, TK[3], ["G1"], tks, eng="pool")
                    tt(hi, T[2], T[3], ALU.add, ["hbfi"], TK[2], TK[3], eng="pool")
                    ppi = 2 * cc + pp
                    xi = {0: 0, 2: 1, 4: 2, 6: 3}.get(ppi, ppi - 4 if ppi >= 8 else None)
                    if extras and xi is not None and xi < len(extras):
                        st8["s5mode"] = True
                        extras[xi]()
                        st8["s5mode"] = False
                    if pp == 0:
                        yb = st8["yb"]
                        st8["yb"] = 13 - yb
                    for j in range(2):
                        q = 2 * pp + j
                        pe(lambda e, bct=bct, q=q, j=j, yb=yb: e.matmul(
                            ps[:, yb, 0:n], lhsT=bct[:, 4 * q + 2, :], rhs=hbf_v[:, j, 0:n], start=(q == 0),
                            stop=False), [bck, "hbfr"], [psk(yb)], ms=False)
                        pe(lambda e, bct=bct, q=q, j=j, yb=yb: e.matmul(
                            ps[:, yb, 0:n], lhsT=bct[:, 4 * q + 3, :], rhs=hbf_v[:, 2 + j, 0:n], start=False,
                            stop=(q == 3)), [bck, "hbfi"], [psk(yb)], ms=(j == 1))
                dve(lambda e, cc=cc, yb=yb: e.scalar_tensor_tensor(
                    out=tmp_v[:, 0, 0:n], in0=uS[:, cc, 0:n], scalar=dsk[:, cc:cc + 1], in1=ps[:, yb, 0:n],
                    op0=ALU.mult, op1=ALU.add), [uK % cc, "dsk", psk(yb)], ["tmp0"])
                actop(lambda e, cc=cc: e.activation(out=ya_v[:, cc, 0:n], in_=tmp_v[:, 0, 0:n],
                                                    func=AF.Gelu_apprx_tanh), ["tmp0"], ["ya"])
            st8["bank"] = 0

        def conv3(xbuf3, acc3, wts, bias, nseg, L, kx, ka, kw, kb):
            actop(lambda e: e.activation(out=acc3, in_=xbuf3[:, :, 2:L + 2], func=AF.Identity,
                                         scale=wts[:, 2:3], bias=bias), [kx, kw, kb], [ka])
            dve(lambda e: e.scalar_tensor_tensor(out=acc3, in0=xbuf3[:, :, 1:L + 1], scalar=wts[:, 1:2], in1=acc3,
                                                  op0=ALU.mult, op1=ALU.add), [kx, kw, ka], [ka])
            dve(lambda e: e.scalar_tensor_tensor(out=acc3, in0=xbuf3[:, :, 0:L], scalar=wts[:, 0:1], in1=acc3,
                                                  op0=ALU.mult, op1=ALU.add), [kx, kw, ka], [ka])

        def block(xsrc, row0, n, nseg, L, mode, ydst=None, SI=None, SO=None, sik="H", sok="H", alt=0):
            hTd, hTk = (hT_alt, "hTb") if alt else (hT, "hT")
            uS, uK = (u_alt, "ub%d") if alt else (u_v, "u%d")
            tts = []
            c = 0
            while c < n:
                R = min(128, n - c)
                tts.append((c, R))
                c += R
            v3 = lambda ap: ap.rearrange("p (s l) -> p s l", s=nseg)
            for ti, (c0, R) in enumerate(tts):
                S.op("act", lambda e, ti=ti, c0=c0, R=R: e.dma_start(
                    out=xt[0:R, ti, :], in_=xsrc[row0 + c0:row0 + c0 + R, :]), writes=["x%d" % ti],
                    chan="x%d" % ti)
            norm_T(tts, n1g, "n1g", hTd, hTk)
            state_only = (mode == "pre")
            for j in range(4):
                bu_ = proj_pair(256 * j, n, hTd, hTk)
                for ci in range(2):
                    cc = 2 * j + ci
                    actop(lambda e, cc=cc, b=bu_[ci]: e.activation(out=uS[:, cc, 0:n], in_=ps[:, b, 0:n],
                                                                    func=AF.Copy), [psk(bu_[ci])], [uK % cc])

            def grp(wb, key, K, c0col, rhs3, rkey):
                bs = (bank(), bank())
                nh = K // 1024
                for h in range(nh):
                    wt, wk = ftile(wb, key, h * 1024, 8, c0col, 256)
                    for ci in range(2):
                        for k in range(8):
                            kc = h * 8 + k
                            pe(lambda e, wt=wt, ci=ci, k=k, kc=kc, bs=bs: e.matmul(
                                ps[:, bs[ci], 0:n], lhsT=wt[:, k, ci * 128:(ci + 1) * 128], rhs=rhs3[:, kc, 0:n],
                                start=(kc == 0), stop=(kc == K // 128 - 1)),
                                [wk, rkey], [psk(bs[ci])], ms=(k == 7))
                return bs

            def mk2b(j):
                def f():
                    bc_ = proj_pair(2048 + 256 * j, n, hTd, hTk)
                    for ci in range(2):
                        actop(lambda e, ci=ci, b=bc_[ci]: e.activation(out=ctmp_v[:, ci, 0:n], in_=ps[:, b, 0:n],
                                                                        func=AF.Copy), [psk(bc_[ci])], ["ctmp%d" % ci])
                    bv_ = proj_pair(3072 + 256 * j, n, hTd, hTk)
                    for ci in range(2):
                        cc = 2 * j + ci
                        xb3 = xsc_v[:, 0:nseg * (L + 2)].rearrange("p (s l) -> p s l", s=nseg)
                        dve(lambda e, cc=cc, xb3=xb3: e.tensor_copy(out=xb3[:, :, 0:2], in_=hs_sc[:, cc, 0:nseg, :]),
                            ["hs_sc"], ["xsc"])
                        dve(lambda e, ci=ci, b=bv_[ci], xb3=xb3: e.tensor_tensor(
                            out=xb3[:, :, 2:L + 2], in0=v3(ctmp_v[:, ci, 0:n]), in1=v3(ps[:, b, 0:n]), op=ALU.mult),
                            ["ctmp%d" % ci, psk(bv_[ci])], ["xsc"])
                        dve(lambda e, cc=cc, xb3=xb3: e.tensor_copy(out=hs_sc[:, cc, 0:nseg, :],
                                                                    in_=xb3[:, :, L:L + 2]),
                            ["xsc"], ["hs_sc"])
                        acc3 = v3(accs_v[:, ci, 0:n])
                        conv3(xb3, acc3, scw[:, cc, :], scb[:, cc:cc + 1], nseg, L, "xsc", "accs%d" % ci, "scw", "scb")
                    bg_ = proj_pair(1024 + 256 * j, n, hTd, hTk)
                    for ci in range(2):
                        cc = 2 * j + ci
                        dve(lambda e, cc=cc, ci=ci, b=bg_[ci]: e.tensor_tensor(
                            out=scin_v[:, cc, 0:n], in0=accs_v[:, ci, 0:n], in1=ps[:, b, 0:n], op=ALU.mult),
                            ["accs%d" % ci, psk(bg_[ci])], ["scin"])
                return f

            def mk4(jj):
                def f():
                    bgb = grp(win_b, "win_b", 2048, 6144 + 256 * jj, hT, "hT")
                    for ci in range(2):
                        actop(lambda e, ci=ci, b=bgb[ci]: e.activation(out=sig_v[:, 4 + ci, 0:n], in_=ps[:, b, 0:n],
                                                                        func=AF.Sigmoid), [psk(bgb[ci])],
                              ["sig%d" % (4 + ci)])
                    bso = grp(wsc_b, "wsc_b", 1024, 256 * jj, scin_v, "scin")
                    for ci in range(2):
                        jm = 2 * jj + ci
                        dve(lambda e, ci=ci, b=bso[ci], jm=jm: e.tensor_tensor(
                            out=mg_v[:, jm, 0:n], in0=ps[:, b, 0:n], in1=sig_v[:, 4 + ci, 0:n], op=ALU.mult),
                            [psk(bso[ci]), "sig%d" % (4 + ci)], ["mg%d" % jm])
                return f

            extras = [] if state_only else [mk2b(j) for j in range(4)] + [mk4(jj) for jj in range(8)]
            s5_phase(n, nseg, L, SI[0], SI[1], SO[0], SO[1], sik, sok, state_only, uS, uK, extras)
            if state_only:
                return
            for jj in range(8):
                bgh = grp(wglu_b, "wglu_b", 1024, 2048 + 256 * jj, ya_v, "ya")
                for ci in range(2):
                    actop(lambda e, ci=ci, b=bgh[ci]: e.activation(out=sig_v[:, ci, 0:n], in_=ps[:, b, 0:n],
                                                                    func=AF.Sigmoid), [psk(bgh[ci])], ["sig%d" % ci])
                bga = grp(win_b, "win_b", 2048, 4096 + 256 * jj, hT, "hT")
                for ci in range(2):
                    actop(lambda e, ci=ci, b=bga[ci]: e.activation(out=sig_v[:, 2 + ci, 0:n], in_=ps[:, b, 0:n],
                                                                    func=AF.Sigmoid), [psk(bga[ci])],
                          ["sig%d" % (2 + ci)])
                bgl = grp(wglu_b, "wglu_b", 1024, 256 * jj, ya_v, "ya")
                for ci in range(2):
                    jm = 2 * jj + ci
                    m1 = tmp_v[:, ci, 0:n]
                    dve(lambda e, ci=ci, b=bgl[ci], m1=m1: e.tensor_tensor(
                        out=m1, in0=ps[:, b, 0:n], in1=sig_v[:, ci, 0:n], op=ALU.mult),
                        [psk(bgl[ci]), "sig%d" % ci], ["tmp%d" % ci])
                    dve(lambda e, ci=ci, m1=m1: e.tensor_tensor(
                        out=m1, in0=m1, in1=sig_v[:, 2 + ci, 0:n], op=ALU.mult), ["tmp%d" % ci, "sig%d" % (2 + ci)],
                        ["tmp%d" % ci])
                    dve(lambda e, jm=jm, m1=m1: e.tensor_tensor(
                        out=mg_v[:, jm, 0:n], in0=mg_v[:, jm, 0:n], in1=m1, op=ALU.add),
                        ["tmp%d" % ci, "mg%d" % jm], ["mg%d" % jm])

            def tok_proj(wb, key, nk, lhs3, lkey):
                for cb in range(4):
                    b4 = bank4()
                    k0 = 0
                    while k0 < nk:
                        kn = min(4, nk - k0)
                        src = wb[128 * k0:128 * (k0 + kn), cb * 512:(cb + 1) * 512].rearrange(
                            "(k p) c -> p k c", p=128)
                        wt, wk = wload(src, kn, 512, key)
                        for kl in range(kn):
                            kc = k0 + kl
                            for ti, (c0, R) in enumerate(tts):
                                pe(lambda e, wt=wt, kl=kl, kc=kc, ti=ti, c0=c0, R=R, b4=b4: e.matmul(
                                    ps[0:R, b4 + ti, :], lhsT=lhs3[:, kc, c0:c0 + R], rhs=wt[:, kl, :],
                                    start=(kc == 0), stop=(kc == nk - 1)),
                                    [wk, (lkey if lkey else "mg%d" % kc)], [psk(b4 + ti)],
                                    ms=(kl == kn - 1 and ti == len(tts) - 1))
                        k0 += kn
                    for ti, (c0, R) in enumerate(tts):
                        dve(lambda e, ti=ti, R=R, cb=cb, b4=b4: e.tensor_tensor(
                            out=xt[0:R, ti, cb * 512:(cb + 1) * 512], in0=xt[0:R, ti, cb * 512:(cb + 1) * 512],
                            in1=ps[0:R, b4 + ti, :], op=ALU.add), ["x%d" % ti, psk(b4 + ti)], ["x%d" % ti])

            tok_proj(wo_b, "wo_b", 16, mg_v, None)
            norm_T(tts, n2g, "n2g", hT, "hT")
            for ip in range(22):
                ncol = 256 if ip < 21 else 128
                nci = ncol // 128
                bvs = [bank() for _ in range(nci)]
                bgs = [bank() for _ in range(nci)]
                for (c0col, bsx) in ((256 * ip, bvs), (5504 + 256 * ip, bgs)):
                    for h in range(2):
                        wt, wk = ftile(wup_b, "wup_b", h * 1024, 8, c0col, ncol)
                        for ci in range(nci):
                            for k in range(8):
                                kc = h * 8 + k
                                pe(lambda e, wt=wt, ci=ci, k=k, kc=kc, bsx=bsx: e.matmul(
                                    ps[:, bsx[ci], 0:n], lhsT=wt[:, k, ci * 128:(ci + 1) * 128], rhs=hT[:, kc, 0:n],
                                    start=(kc == 0), stop=(kc == KC - 1)),
                                    [wk, "hT"], [psk(bsx[ci])], ms=(k == 7))
                for ci in range(nci):
                    i = 2 * ip + ci
                    fs = st8["ff"]
                    st8["ff"] = 1 - fs
                    for w, (bsx, ch) in enumerate(((bvs, i), (bgs, 43 + i))):
                        xk = "xff%d" % (2 * fs + w)
                        ak = "acf%d" % (2 * fs + w)
                        xb3 = xff_v[:, 2 * fs + w, 0:nseg * (L + 2)].rearrange("p (s l) -> p s l", s=nseg)
                        acc3 = v3(acf_v[:, 2 * fs + w, 0:n])
                        dve(lambda e, ch=ch, xb3=xb3: e.tensor_copy(out=xb3[:, :, 0:2], in_=hs_ff[:, ch, 0:nseg, :]),
                            ["hs_ff"], [xk])
                        actop(lambda e, b=bsx[ci], xb3=xb3: e.activation(out=xb3[:, :, 2:L + 2], in_=v3(ps[:, b, 0:n]),
                                                                          func=AF.Copy), [psk(bsx[ci])], [xk])
                        dve(lambda e, ch=ch, xb3=xb3: e.tensor_copy(out=hs_ff[:, ch, 0:nseg, :], in_=xb3[:, :, L:L + 2]),
                            [xk], ["hs_ff"])
                        conv3(xb3, acc3, ffw[:, ch, :], ffb[:, ch:ch + 1], nseg, L, xk, ak, "ffw", "ffb")
                    ag = acf_v[:, 2 * fs + 1, 0:n]
                    av = acf_v[:, 2 * fs + 0, 0:n]
                    actop(lambda e, ag=ag: e.activation(out=ag, in_=ag, func=AF.Silu),
                          ["acf%d" % (2 * fs + 1)], ["acf%d" % (2 * fs + 1)])
                    dve(lambda e, i=i, ag=ag, av=av: e.tensor_tensor(out=act_v[:, i, 0:n], in0=ag, in1=av, op=ALU.mult),
                        ["acf%d" % (2 * fs + 1), "acf%d" % (2 * fs)], ["actv"])
            tok_proj(wdn_b, "wdn_b", 43, act_v, "actv")
            for ti, (c0, R) in enumerate(tts):
                xk = "x%d" % ti
                bj = bank4()
                pj = ps[:, bj:bj + 4, :].rearrange("p a b -> p (a b)")
                actop(lambda e, ti=ti, R=R, pj=pj: e.activation(
                    out=pj[0:R, :], in_=xt[0:R, ti, :], func=AF.Square, accum_out=ss[0:R, ti:ti + 1]),
                    [xk], [psk(bj), psk(bj + 1), psk(bj + 2), psk(bj + 3), "ss%d" % ti])
                actop(lambda e, ti=ti, R=R: e.activation(
                    out=rstd[0:R, ti:ti + 1], in_=ss[0:R, ti:ti + 1], func=AF.Sqrt, scale=1.0 / D, bias=EPS),
                    ["ss%d" % ti], ["rstd%d" % ti])
                dve(lambda e, ti=ti, R=R: e.reciprocal(out=rstd[0:R, ti:ti + 1], in_=rstd[0:R, ti:ti + 1]),
                    ["rstd%d" % ti], ["rstd%d" % ti])
                dve(lambda e, ti=ti, R=R: e.scalar_tensor_tensor(
                    out=ystage[0:R, :], in0=xt[0:R, ti, :], scalar=rstd[0:R, ti:ti + 1], in1=fg[0:R, :],
                    op0=ALU.mult, op1=ALU.mult), [xk, "rstd%d" % ti, "fg"], ["hb0", "hb1"])
                S.op("act", lambda e, c0=c0, R=R: e.dma_start(out=ydst[row0 + c0:row0 + c0 + R, :],
                                                                in_=ystage[0:R, :]),
                     reads=["hb0", "hb1"], chan="yout")

        Hv = (Hr[:].unsqueeze(1), Hi[:].unsqueeze(1))
        for bi_, r0 in enumerate(range(0, HALF, 512)):
            block(xprev, r0, 512, 1, 512, "pre", SI=Hv, SO=Hv, alt=bi_ % 2)
        S.drain_key("act", "cout0")
        S.drain_key("act", "cout1")
        S.drain_key("dve", "cin0")
        S.drain_key("dve", "cin1")
        dve(lambda e: e.tensor_copy(out=SIr[:, 0, :], in_=Hr[:]), ["H"], ["SI"])
        dve(lambda e: e.tensor_copy(out=SIi[:, 0, :], in_=Hi[:]), ["H"], ["SI"])
        block(xsmall, 0, 80, 5, 16, "full", ydst=y_small, SI=(SIr[:], SIi[:]), SO=(SOr[:], SOi[:]), sik="SI",
              sok="SO")
        dve(lambda e: e.tensor_copy(out=Hr[:], in_=SOr[:, 0, :]), ["SO"], ["H"])
        dve(lambda e: e.tensor_copy(out=Hi[:], in_=SOi[:, 0, :]), ["SO"], ["H"])

        sh5 = [128, 5, 32]
        fb5 = lambda t: t[:].unsqueeze(1).to_broadcast(sh5)
        xA = SIr
        xB = SIi
        dve(lambda e: e.tensor_tensor(out=xA[:], in0=SOr[:], in1=fb5(fr_t), op=ALU.mult), ["SO", "SI"], ["SI"])
        dve(lambda e: e.tensor_tensor(out=xB[:], in0=SOi[:], in1=fb5(fi_t), op=ALU.mult), ["SO", "SI"], ["SI"])
        dve(lambda e: e.tensor_tensor(out=s5st[:, 0, 0:5, :], in0=xA[:], in1=xB[:], op=ALU.subtract), ["SI"],
            ["s5st"])
        dve(lambda e: e.tensor_tensor(out=xA[:], in0=SOr[:], in1=fb5(fi_t), op=ALU.mult), ["SO", "SI"], ["SI"])
        dve(lambda e: e.tensor_tensor(out=xB[:], in0=SOi[:], in1=fb5(fr_t), op=ALU.mult), ["SO", "SI"], ["SI"])
        dve(lambda e: e.tensor_tensor(out=s5st[:, 1, 0:5, :], in0=xA[:], in1=xB[:], op=ALU.add), ["SI"], ["s5st"])
        S.op("sp", lambda e: e.dma_start(out=sco_d[:, :, 0:5, :], in_=hs_sc[:]), reads=["hs_sc"], chan="o_sc")
        S.op("sp", lambda e: e.dma_start(out=ffo_d[:, :, 0:5, :], in_=hs_ff[:]), reads=["hs_ff"], chan="o_ff")
        for bi in range(NBLK):
            block(xown, bi * 512, 512, 1, 512, "full", ydst=y_own, SI=Hv, SO=Hv)
        xA2 = SIr[:, 0, :]
        xB2 = SIi[:, 0, :]
        dve(lambda e: e.tensor_tensor(out=xA2, in0=Hr[:], in1=fr_t[:], op=ALU.mult), ["H", "SI"], ["SI"])
        dve(lambda e: e.tensor_tensor(out=xB2, in0=Hi[:], in1=fi_t[:], op=ALU.mult), ["H", "SI"], ["SI"])
        dve(lambda e: e.tensor_tensor(out=s5st[:, 0, 5, :], in0=xA2, in1=xB2, op=ALU.subtract), ["SI"], ["s5st"])
        dve(lambda e: e.tensor_tensor(out=xA2, in0=Hr[:], in1=fi_t[:], op=ALU.mult), ["H", "SI"], ["SI"])
        dve(lambda e: e.tensor_tensor(out=xB2, in0=Hi[:], in1=fr_t[:], op=ALU.mult), ["H", "SI"], ["SI"])
        dve(lambda e: e.tensor_tensor(out=s5st[:, 1, 5, :], in0=xA2, in1=xB2, op=ALU.add), ["SI"], ["s5st"])

        S.op("sp", lambda e: e.dma_start(out=s5o_d, in_=s5st[:]), reads=["s5st"], chan="o_s5")
        S.op("sp", lambda e: e.dma_start(out=sco_d[:, :, 5, :], in_=hs_sc[:, :, 0, :]), reads=["hs_sc"], chan="o_sc")
        S.op("sp", lambda e: e.dma_start(out=ffo_d[:, :, 5, :], in_=hs_ff[:, :, 0, :]), reads=["hs_ff"], chan="o_ff")
        S.final_wait("sp")
        S.emit()
        build.nops = S.nops
    return nc


_CACHE = {}


def _layout_common(inp):
    f = np.float32
    A = np.ascontiguousarray
    out = {}
    out["w_in"] = A(inp["w_in"][0], dtype=f)
    out["w_glu"] = A(inp["w_glu"][0], dtype=f)
    out["w_sc"] = A(inp["w_sc_out"][0], dtype=f)
    out["w_o"] = A(inp["w_o"][0], dtype=f)
    out["w_up"] = A(inp["w_up"][0], dtype=f)
    out["w_dn"] = A(inp["w_down"][0], dtype=f)
    bpad = np.zeros((2, 32, 128, 128), f)
    cpad = np.zeros((2, 32, 128, 128), f)
    for w, (bb, cc_) in enumerate(((inp["b_re"][0], inp["c_re"][0]), (inp["b_im"][0], inp["c_im"][0]))):
        for pr in range(32):
            for j in range(2):
                g = 2 * pr + j
                r0 = 32 * (pr % 4) + 16 * j
                bpad[w, pr, r0:r0 + 16, 64 * j:64 * j + 64] = bb[g].T
                cpad[w, pr, 64 * j:64 * j + 64, r0:r0 + 16] = cc_[g].T
    out["bpad"] = bpad
    out["cpad"] = cpad

    def pairlay(a):
        return A(a.reshape(32, 2, 64).transpose(1, 2, 0).reshape(128, 32), dtype=f)

    out["lamre"] = pairlay(inp["lam_re"][0])
    out["lamim"] = pairlay(inp["lam_im"][0])
    out["logdt"] = pairlay(np.broadcast_to(inp["log_dt"][0][:, None], (64, 64)))
    out["dskip"] = A(inp["d_skip"][0].reshape(8, 128).T, dtype=f)
    out["scw"] = A(inp["sc_conv_w"][0].reshape(3, 8, 128).transpose(2, 1, 0), dtype=f)
    out["scb"] = A(inp["sc_conv_b"][0].reshape(8, 128).T, dtype=f)
    out["ffw"] = A(inp["ffn_conv_w"][0].reshape(3, 86, 128).transpose(2, 1, 0), dtype=f)
    out["ffb"] = A(inp["ffn_conv_b"][0].reshape(86, 128).T, dtype=f)
    out["n1g"] = A(inp["norm1_g"][0].reshape(16, 128).T, dtype=f)
    out["n2g"] = A(inp["norm2_g"][0].reshape(16, 128).T, dtype=f)
    out["fg"] = A(np.broadcast_to(inp["final_norm_g"][None, :], (128, D)), dtype=f)
    return out


def kernel(**inp):
    f = np.float32
    A = np.ascontiguousarray
    xp = np.asarray(inp["x_prompt"])
    xs = np.asarray(inp["x_sample"])
    B, SEQ, _ = xp.shape
    HALF = SEQ // 2
    PRE = HALF - 16
    assert B == 4 and xs.shape[0] == 32 and xs.shape[1] == 16
    inp = {k: np.asarray(v) for k, v in inp.items()}
    common = _layout_common(inp)
    in_maps = []
    for c in range(8):
        b, half = c // 2, c % 2
        m = dict(common)
        if half == 0:
            prev = np.zeros((HALF, D), f)
        else:
            prev = xp[b, 0:HALF]
        m["xprev"] = A(np.concatenate([np.zeros((16, D), f), prev[0:PRE]], 0), dtype=f)
        m["xsmall"] = A(np.concatenate([prev[PRE:HALF], xs[4 * c:4 * c + 4].reshape(64, D)], 0), dtype=f)
        m["xown"] = A(xp[b, half * HALF:(half + 1) * HALF], dtype=f)
        sl = slice(4 * c, 4 * c + 4)
        for nm, src in (("sre", inp["state_s5_re"][0, sl]), ("sim", inp["state_s5_im"][0, sl])):
            m[nm] = A(src.reshape(4, 32, 2, 64).transpose(2, 3, 0, 1).reshape(128, 4, 32), dtype=f)
        m["csc"] = A(inp["cache_sc_conv"][0, sl].reshape(4, 2, 8, 128).transpose(3, 2, 0, 1), dtype=f)
        m["cff"] = A(inp["cache_ffn_conv"][0, sl].reshape(4, 2, 86, 128).transpose(3, 2, 0, 1), dtype=f)
        in_maps.append(m)
    if HALF not in _CACHE:
        _CACHE[HALF] = build(HALF)
    nc = _CACHE[HALF]
    res = run_bass_kernel_spmd(nc, in_maps, core_ids=list(range(8))).results

    y_prompt = np.zeros((4, SEQ, D), f)
    y_sample = np.zeros((32, 16, D), f)
    p_re = np.zeros((1, 4, 64, 64), f)
    p_im = np.zeros((1, 4, 64, 64), f)
    p_sc = np.zeros((1, 4, 2, 1024), f)
    p_ff = np.zeros((1, 4, 2, 11008), f)
    s_re = np.zeros((1, 32, 64, 64), f)
    s_im = np.zeros((1, 32, 64, 64), f)
    s_sc = np.zeros((1, 32, 2, 1024), f)
    s_ff = np.zeros((1, 32, 2, 11008), f)

    def unpair(a):
        return a.reshape(2, 64, 32).transpose(2, 0, 1).reshape(64, 64)

    def uncache(a):
        return a.transpose(2, 1, 0).reshape(2, -1)

    for c in range(8):
        b, half = c // 2, c % 2
        r = res[c]
        y_prompt[b, half * HALF:(half + 1) * HALF] = r["y_own"]
        y_sample[4 * c:4 * c + 4] = r["y_small"][16:80].reshape(4, 16, D)
        for s in range(4):
            s_re[0, 4 * c + s] = unpair(r["s5o"][:, 0, 1 + s, :])
            s_im[0, 4 * c + s] = unpair(r["s5o"][:, 1, 1 + s, :])
            s_sc[0, 4 * c + s] = uncache(r["sco"][:, :, 1 + s, :])
            s_ff[0, 4 * c + s] = uncache(r["ffo"][:, :, 1 + s, :])
        if half == 1:
            p_re[0, b] = unpair(r["s5o"][:, 0, 5, :])
            p_im[0, b] = unpair(r["s5o"][:, 1, 5, :])
            p_sc[0, b] = uncache(r["sco"][:, :, 5, :])
            p_ff[0, b] = uncache(r["ffo"][:, :, 5, :])
    return (y_prompt, y_sample, p_re, p_im, p_sc, p_ff, s_re, s_im, s_sc, s_ff)
```

```python
import numpy as np
from contextlib import ExitStack
import concourse.bass as bass
import concourse.mybir as mybir
from concourse.bass_utils import run_bass_kernel_spmd

F32 = mybir.dt.float32
BF16 = mybir.dt.bfloat16
ALU = mybir.AluOpType
AF = mybir.ActivationFunctionType

D = 2048
KC = 16
NCC = 8
NFF = 43
NUP = 86
NPAIR = 32
PI = float(np.pi)
PI_SAFE = 3.1415925
C1 = float(np.float32(2 * np.pi))
C2 = float(2 * np.pi - np.float64(np.float32(2 * np.pi)))
EPS = 1e-6
ENGS = ("pe", "act", "dve", "pool", "sp")


class Sched:
    def __init__(self, nc, es):
        self.nc = nc
        self.es = es
        self.sems = {}
        self.cnt = {}
        self.e = {}
        for n in ENGS:
            self._sem("prog_" + n)
            self.e[n] = dict(sem="prog_" + n, waited={}, ops=[], pr=[], pw=[])
        self.tiles = {}
        self.nops = 0

    def _sem(self, name):
        if name not in self.sems:
            self.sems[name] = self.es.enter_context(self.nc.semaphore(name))
            self.cnt[name] = 0
        return name

    def op(self, eng, fn, reads=(), writes=(), ms=True, chan=None):
        e = self.e[eng]
        need = {}

        def add(d):
            for s, v in d.items():
                if need.get(s, 0) < v:
                    need[s] = v

        for k in reads:
            t = self.tiles.get(k)
            if t:
                add(t[0])
        for k in writes:
            t = self.tiles.get(k)
            if t:
                add(t[0])
                add(t[1])
        waits = []
        for s, v in need.items():
            if eng == "pe" and s == e["sem"]:
                continue
            if e["waited"].get(s, 0) < v:
                e["waited"][s] = v
                waits.append((self.sems[s], v))
        self.nops += 1
        if chan is not None:
            s = self._sem("ch_" + chan)
            self.cnt[s] += 16
            stamp = (s, self.cnt[s])
            inc = (self.sems[s], 16)
        elif ms:
            s = e["sem"]
            self.cnt[s] += 1
            stamp = (s, self.cnt[s])
            inc = (self.sems[s], 1)
        else:
            stamp = None
            inc = None

        def run(engine, waits=waits, fn=fn, inc=inc):
            for s, v in waits:
                engine.wait_ge(s, v)
            ins = fn(engine)
            if inc is not None:
                ins.then_inc(inc[0], inc[1])

        e["ops"].append(run)
        if stamp is None:
            e["pr"] += list(reads)
            e["pw"] += list(writes)
        else:
            allr = list(reads) + e["pr"]
            allw = list(writes) + e["pw"]
            e["pr"] = []
            e["pw"] = []
            for k in allw:
                self.tiles[k] = [{stamp[0]: stamp[1]}, {}]
            for k in allr:
                t = self.tiles.setdefault(k, [{}, {}])
                if t[1].get(stamp[0], 0) < stamp[1]:
                    t[1][stamp[0]] = stamp[1]
        return stamp

    def drain_key(self, eng, key):
        e = self.e[eng]
        t = self.tiles.get(key)
        waits = []
        if t:
            for d in t:
                for sname, v in d.items():
                    waits.append((self.sems[sname], v))

        def run(engine, waits=waits):
            for s_, v in waits:
                engine.wait_ge(s_, v)

        e["ops"].append(run)

    def final_wait(self, eng):
        e = self.e[eng]
        waits = [(self.sems[s], c) for s, c in self.cnt.items() if c > 0 and s != e["sem"]]

        def run(engine, waits=waits):
            for s, v in waits:
                engine.wait_ge(s, v)

        e["ops"].append(run)

    def emit(self):
        nc = self.nc
        with nc.Block() as block:
            @block.tensor
            def _(eng):
                for f in self.e["pe"]["ops"]:
                    f(eng)

            @block.scalar
            def _(eng):
                for f in self.e["act"]["ops"]:
                    f(eng)

            @block.vector
            def _(eng):
                for f in self.e["dve"]["ops"]:
                    f(eng)

            @block.gpsimd
            def _(eng):
                for f in self.e["pool"]["ops"]:
                    f(eng)

            @block.sync
            def _(eng):
                for f in self.e["sp"]["ops"]:
                    f(eng)
        for n in ENGS:
            self.e[n]["ops"] = []


def build(HALF):
    NBLK = HALF // 512
    PRE = HALF - 16
    nc = bass.Bass("TRN2", target_bir_lowering=False)

    def din(name, shape):
        return nc.dram_tensor(name, list(shape), F32, kind="ExternalInput").ap()

    def dout(name, shape):
        return nc.dram_tensor(name, list(shape), F32, kind="ExternalOutput").ap()

    xprev = din("xprev", [HALF, D])
    xsmall = din("xsmall", [80, D])
    xown = din("xown", [HALF, D])
    w_in = din("w_in", [D, 8192])
    w_glu = din("w_glu", [1024, 4096])
    w_sc = din("w_sc", [1024, 2048])
    w_o = din("w_o", [2048, 2048])
    w_up = din("w_up", [2048, 11008])
    w_dn = din("w_dn", [5504, 2048])
    bpad = din("bpad", [2, 32, 128, 128])
    cpad = din("cpad", [2, 32, 128, 128])
    lamre_d = din("lamre", [128, 32])
    lamim_d = din("lamim", [128, 32])
    logdt_d = din("logdt", [128, 32])
    sre_d = din("sre", [128, 4, 32])
    sim_d = din("sim", [128, 4, 32])
    dskip_d = din("dskip", [128, 8])
    scw_d = din("scw", [128, 8, 3])
    scb_d = din("scb", [128, 8])
    ffw_d = din("ffw", [128, 86, 3])
    ffb_d = din("ffb", [128, 86])
    n1g_d = din("n1g", [128, 16])
    n2g_d = din("n2g", [128, 16])
    fg_d = din("fg", [128, D])
    csc_d = din("csc", [128, 8, 4, 2])
    cff_d = din("cff", [128, 86, 4, 2])

    y_own = dout("y_own", [HALF, D])
    y_small = dout("y_small", [80, D])
    s5o_d = dout("s5o", [128, 2, 6, 32])
    sco_d = dout("sco", [128, 8, 6, 2])
    ffo_d = dout("ffo", [128, 86, 6, 2])

    win_b = nc.dram_tensor("win_b", [D, 8192], BF16).ap()
    wglu_b = nc.dram_tensor("wglu_b", [1024, 4096], BF16).ap()
    wsc_b = nc.dram_tensor("wsc_b", [1024, 2048], BF16).ap()
    wo_b = nc.dram_tensor("wo_b", [2048, 2048], BF16).ap()
    wup_b = nc.dram_tensor("wup_b", [2048, 11008], BF16).ap()
    wdn_b = nc.dram_tensor("wdn_b", [5504, 2048], BF16).ap()
    bc_b = nc.dram_tensor("bc_b", [8, 128, 4, 4, 128], BF16).ap()
    te_d = nc.dram_tensor("te_d", [32, 128, 2, 512], F32).ap()
    tw_d = nc.dram_tensor("tw_d", [32, 128, 2, 512], F32).ap()

    with ExitStack() as es:
        S = Sched(nc, es)

        def sb(name, shape, dt, stack=es):
            return stack.enter_context(nc.sbuf_tensor("s_" + name, list(shape), dt))

        ident = sb("ident", [128, 128], BF16)
        n1g = sb("n1g", [128, 16], F32)
        n2g = sb("n2g", [128, 16], F32)
        fg = sb("fg", [128, D], F32)
        r_t = sb("r_t", [128, 32], F32)
        fr_t = sb("fr_t", [128, 32], F32)
        fi_t = sb("fi_t", [128, 32], F32)
        dsk = sb("dsk", [128, 8], F32)
        scw = sb("scw", [128, 8, 3], F32)
        scb = sb("scb", [128, 8], F32)
        ffw = sb("ffw", [128, 86, 3], F32)
        ffb = sb("ffb", [128, 86], F32)
        Hr = sb("Hr", [128, 32], F32)
        Hi = sb("Hi", [128, 32], F32)
        SIr = sb("SIr", [128, 5, 32], F32)
        SIi = sb("SIi", [128, 5, 32], F32)
        SOr = sb("SOr", [128, 5, 32], F32)
        SOi = sb("SOi", [128, 5, 32], F32)
        s5st = sb("s5st", [128, 2, 6, 32], F32)
        hs_sc = sb("hs_sc", [128, 8, 5, 2], F32)
        hs_ff = sb("hs_ff", [128, 86, 5, 2], F32)
        tiny = sb("tiny", [128, 4, 16], F32)
        rL = sb("rL", [128, 32], F32)
        clT = sb("clT", [128, 32], F32)
        slT = sb("slT", [128, 32], F32)
        accp = sb("accp", [128, 32, 4], F32)
        gtr = sb("gtr", [128, 32], F32)
        gti = sb("gti", [128, 32], F32)
        gtt = sb("gtt", [128, 32], F32)
        ss = sb("ss", [128, 4], F32)
        rstd = sb("rstd", [128, 4], F32)

        with ExitStack() as ses:
            WK = {}

            def cast_w(src, dst, R, C, name, c_lo=0, c_hi=None):
                c_hi = C if c_hi is None else c_hi
                W = c_hi - c_lo
                a = 1
                while W // a > 2048 or W % a:
                    a += 1
                s3 = src[:, c_lo:c_hi].rearrange("r (a c) -> r a c", a=a)
                d3 = dst[:, c_lo:c_hi].rearrange("r (a c) -> r a c", a=a)
                step = max(1, 2048 // a)
                i = 0
                keys = []
                for r0 in range(0, R, step):
                    r1 = min(R, r0 + step)
                    S.op("pool", lambda e, r0=r0, r1=r1: e.dma_start(out=d3[r0:r1], in_=s3[r0:r1]),
                         writes=["%s_%d" % (name, i)], chan=name + str(i % 4))
                    keys.append("%s_%d" % (name, i))
                    i += 1
                WK[name] = keys

            cast_w(w_in, win_b, 2048, 8192, "win_u", 0, 1024)
            for w in range(2):
                for cc in range(8):
                    S.op("pool", lambda e, w=w, cc=cc: e.dma_start(
                        out=bc_b[cc, :, :, w, :], in_=bpad[w, 4 * cc:4 * cc + 4].rearrange("q p n -> p q n")),
                        writes=["bc_b_%d_%d" % (w, cc)], chan="bcb%d" % w)
            WK["bc_b"] = ["bc_b_%d_%d" % (w, cc) for w in range(2) for cc in range(8)] + ["bc_b"]
            cast_w(w_in, win_b, 2048, 8192, "win_b", 1024, 8192)

            def cload(dst_ap, src_ap, key):
                S.op("sp", lambda e: e.dma_start(out=dst_ap, in_=src_ap), writes=[key], chan="c_" + key)

            lamre = sb("lamre", [128, 32], F32, ses)
            lamim = sb("lamim", [128, 32], F32, ses)
            logdt = sb("logdt", [128, 32], F32, ses)
            sre = sb("sre", [128, 4, 32], F32, ses)
            sim = sb("sim", [128, 4, 32], F32, ses)
            cload(lamre[:], lamre_d, "lamre")
            cload(lamim[:], lamim_d, "lamim")
            cload(logdt[:], logdt_d, "logdt")
            cload(sre[:], sre_d, "sre")
            cload(sim[:], sim_d, "sim")
            cload(dsk[:], dskip_d, "dsk")
            cload(scw[:], scw_d, "scw")
            cload(scb[:], scb_d, "scb")
            cload(ffw[:], ffw_d, "ffw")
            cload(ffb[:], ffb_d, "ffb")
            cload(n1g[:], n1g_d, "n1g")
            cload(n2g[:], n2g_d, "n2g")
            cload(fg[:], fg_d, "fg")
            cload(hs_sc[:, :, 1:5, :], csc_d, "hs_sc")
            cload(hs_ff[:, :, 1:5, :], cff_d, "hs_ff")

            identf = sb("identf", [128, 128], F32, ses)
            S.op("pool", lambda e: e.memset(identf[:], 0.0), writes=["identf"])
            S.op("pool", lambda e: e.affine_select(out=identf[:], in_=identf[:], pattern=[[-1, 128]],
                                                      compare_op=ALU.not_equal, fill=1.0, base=0,
                                                      channel_multiplier=1),
                 reads=["identf"], writes=["identf"])
            S.op("dve", lambda e: e.tensor_copy(out=ident[:], in_=identf[:]), reads=["identf"], writes=["ident"])
            S.op("dve", lambda e: e.memset(Hr[:], 0.0), writes=["Hr"])
            S.op("dve", lambda e: e.memset(Hi[:], 0.0), writes=["Hi"])
            S.op("dve", lambda e: e.memset(hs_sc[:, :, 0, :], 0.0), writes=["hs_sc"])
            S.op("dve", lambda e: e.memset(hs_ff[:, :, 0, :], 0.0), writes=["hs_ff"])

            def st(name):
                return sb(name, [128, 32], F32, ses)

            dt_t, th, lrdt, c1, s1, ar, ai, kk, red, t1, t2, den, finr, fini = [st(n) for n in (
                "dt_t", "th", "lrdt", "c1", "s1", "ar", "ai", "kk", "red", "t1", "t2", "den", "finr", "fini")]

            def dve(fn, reads, writes):
                S.op("dve", fn, reads=reads, writes=writes)

            def actop(fn, reads, writes):
                S.op("act", fn, reads=reads, writes=writes)

            def tt(out, a, b, op, ko, ka, kb):
                dve(lambda e: e.tensor_tensor(out=out, in0=a, in1=b, op=op), [ka, kb], [ko])

            actop(lambda e: e.activation(out=dt_t[:], in_=logdt[:], func=AF.Exp), ["logdt"], ["dt_t"])
            tt(th[:], lamim[:], dt_t[:], ALU.mult, "th", "lamim", "dt_t")
            tt(lrdt[:], lamre[:], dt_t[:], ALU.mult, "lrdt", "lamre", "dt_t")
            actop(lambda e: e.activation(out=r_t[:], in_=lrdt[:], func=AF.Exp), ["lrdt"], ["r_t"])

            def sin_of(out, shift, ko):
                dve(lambda e: e.tensor_scalar(out=red[:], in0=th[:], scalar1=shift, scalar2=None, op0=ALU.add),
                    ["th"], ["red"])
                dve(lambda e: e.tensor_scalar(out=kk[:], in0=red[:], scalar1=PI, scalar2=None, op0=ALU.is_gt),
                    ["red"], ["kk"])
                for m in (3, 5, 7, 9, 11):
                    dve(lambda e, m=m: e.scalar_tensor_tensor(out=kk[:], in0=red[:], scalar=m * PI, in1=kk[:],
                                                               op0=ALU.is_gt, op1=ALU.add), ["red", "kk"], ["kk"])
                dve(lambda e: e.scalar_tensor_tensor(out=red[:], in0=kk[:], scalar=-C1, in1=red[:],
                                                      op0=ALU.mult, op1=ALU.add), ["kk", "red"], ["red"])
                dve(lambda e: e.scalar_tensor_tensor(out=red[:], in0=kk[:], scalar=-C2, in1=red[:],
                                                      op0=ALU.mult, op1=ALU.add), ["kk", "red"], ["red"])
                dve(lambda e: e.tensor_scalar(out=red[:], in0=red[:], scalar1=-PI_SAFE, scalar2=PI_SAFE,
                                                op0=ALU.max, op1=ALU.min), ["red"], ["red"])
                actop(lambda e: e.activation(out=out, in_=red[:], func=AF.Sin), ["red"], [ko])

            sin_of(s1[:], 0.0, "s1")
            sin_of(c1[:], PI / 2, "c1")
            tt(ar[:], r_t[:], c1[:], ALU.mult, "ar", "r_t", "c1")
            tt(ai[:], r_t[:], s1[:], ALU.mult, "ai", "r_t", "s1")
            dve(lambda e: e.tensor_scalar(out=ar[:], in0=ar[:], scalar1=-1.0, scalar2=None, op0=ALU.add),
                ["ar"], ["ar"])
            tt(t1[:], lamre[:], lamre[:], ALU.mult, "t1", "lamre", "lamre")
            tt(t2[:], lamim[:], lamim[:], ALU.mult, "t2", "lamim", "lamim")
            tt(den[:], t1[:], t2[:], ALU.add, "den", "t1", "t2")
            dve(lambda e: e.reciprocal(out=den[:], in_=den[:]), ["den"], ["den"])
            tt(t1[:], ar[:], lamre[:], ALU.mult, "t1", "ar", "lamre")
            tt(t2[:], ai[:], lamim[:], ALU.mult, "t2", "ai", "lamim")
            tt(t1[:], t1[:], t2[:], ALU.add, "t1", "t1", "t2")
            tt(fr_t[:], t1[:], den[:], ALU.mult, "fr_t", "t1", "den")
            tt(t1[:], ai[:], lamre[:], ALU.mult, "t1", "ai", "lamre")
            tt(t2[:], ar[:], lamim[:], ALU.mult, "t2", "ar", "lamim")
            tt(t1[:], t1[:], t2[:], ALU.subtract, "t1", "t1", "t2")
            tt(fi_t[:], t1[:], den[:], ALU.mult, "fi_t", "t1", "den")
            tt(t1[:], fr_t[:], fr_t[:], ALU.mult, "t1", "fr_t", "fr_t")
            tt(t2[:], fi_t[:], fi_t[:], ALU.mult, "t2", "fi_t", "fi_t")
            tt(t1[:], t1[:], t2[:], ALU.add, "t1", "t1", "t2")
            dve(lambda e: e.reciprocal(out=t1[:], in_=t1[:]), ["t1"], ["t1"])
            tt(finr[:], fr_t[:], t1[:], ALU.mult, "finr", "fr_t", "t1")
            dve(lambda e: e.scalar_tensor_tensor(out=fini[:], in0=fi_t[:], scalar=-1.0, in1=t1[:],
                                                  op0=ALU.mult, op1=ALU.mult), ["fi_t", "t1"], ["fini"])
            sA = sb("sA", [128, 4, 32], F32, ses)
            sB = sb("sB", [128, 4, 32], F32, ses)
            bc4 = lambda t: t[:].unsqueeze(1).to_broadcast([128, 4, 32])
            tt(sA[:], sre[:], bc4(finr), ALU.mult, "sA", "sre", "finr")
            tt(sB[:], sim[:], bc4(fini), ALU.mult, "sB", "sim", "fini")
            tt(SIr[:, 1:5, :], sA[:], sB[:], ALU.subtract, "SIr", "sA", "sB")
            tt(sA[:], sre[:], bc4(fini), ALU.mult, "sA", "sre", "fini")
            tt(sB[:], sim[:], bc4(finr), ALU.mult, "sB", "sim", "finr")
            tt(SIi[:, 1:5, :], sA[:], sB[:], ALU.add, "SIi", "sA", "sB")

            with ExitStack() as cs:
                Cr = sb("Cr_s", [128, 32, 128], F32, cs)
                Ci = sb("Ci_s", [128, 32, 128], F32, cs)
                Ta = sb("Ta_s", [128, 32, 128], F32, cs)
                Tb = sb("Tb_s", [128, 32, 128], F32, cs)
                Cst = sb("Cst_s", [128, 32, 2, 128], BF16, cs)
                cload(Cr[:], cpad[0].rearrange("pr p n -> p pr n"), "Cr")
                cload(Ci[:], cpad[1].rearrange("pr p n -> p pr n"), "Ci")
                bcf = lambda t: t[:].unsqueeze(2).to_broadcast([128, 32, 128])
                tt(Ta[:], Cr[:], bcf(fr_t), ALU.mult, "Ta", "Cr", "fr_t")
                tt(Tb[:], Ci[:], bcf(fi_t), ALU.mult, "Tb", "Ci", "fi_t")
                tt(Cst[:, :, 0, :], Ta[:], Tb[:], ALU.subtract, "Cst", "Ta", "Tb")
                tt(Ta[:], Cr[:], bcf(fi_t), ALU.mult, "Ta", "Cr", "fi_t")
                tt(Tb[:], Ci[:], bcf(fr_t), ALU.mult, "Tb", "Ci", "fr_t")
                dve(lambda e: e.scalar_tensor_tensor(out=Cst[:, :, 1, :], in0=Ta[:], scalar=-1.0, in1=Tb[:],
                                                      op0=ALU.mult, op1=ALU.subtract), ["Ta", "Tb"], ["Cst"])
                for cc in range(8):
                    S.op("sp", lambda e, cc=cc: e.dma_start(
                        out=bc_b[cc, :, :, 2:4, :], in_=Cst[:, 4 * cc:4 * cc + 4, :, :]),
                        reads=["Cst"], writes=["bc_b"], chan="bcc")
                S.drain_key("sp", "Cst")
                S.emit()

            Ec = sb("Ec", [128, 16, 512], F32, ses)
            Es = sb("Es", [128, 16, 512], F32, ses)
            Xa = sb("Xa", [128, 16, 256], F32, ses)
            Xb = sb("Xb", [128, 16, 256], F32, ses)
            Rp = sb("Rp", [128, 16, 512], F32, ses)
            rpw = sb("rpw", [128, 16], F32, ses)
            for hf in range(2):
                p0 = hf * 16
                dve(lambda e, p0=p0: e.tensor_copy(out=Ec[:, :, 0], in_=c1[:, p0:p0 + 16]), ["c1"], ["Ec"])
                dve(lambda e, p0=p0: e.tensor_copy(out=Es[:, :, 0], in_=s1[:, p0:p0 + 16]), ["s1"], ["Es"])
                n = 1
                while n < 512:
                    cm = Ec[:, :, n - 1:n].to_broadcast([128, 16, n])
                    sm = Es[:, :, n - 1:n].to_broadcast([128, 16, n])
                    xa = Xa[:, :, 0:n]
                    xb = Xb[:, :, 0:n]
                    tt(xa, Ec[:, :, 0:n], cm, ALU.mult, "Xa", "Ec", "Ec")
                    tt(xb, Es[:, :, 0:n], sm, ALU.mult, "Xb", "Es", "Es")
                    tt(Ec[:, :, n:2 * n], xa, xb, ALU.subtract, "Ec", "Xa", "Xb")
                    tt(xa, Ec[:, :, 0:n], sm, ALU.mult, "Xa", "Ec", "Es")
                    tt(xb, Es[:, :, 0:n], cm, ALU.mult, "Xb", "Es", "Ec")
                    tt(Es[:, :, n:2 * n], xa, xb, ALU.add, "Es", "Xa", "Xb")
                    n *= 2
                S.op("sp", lambda e, p0=p0: e.dma_start(
                    out=te_d[p0:p0 + 16, :, 0, :].rearrange("pr p t -> p pr t"), in_=Ec[:]),
                    reads=["Ec"], writes=["te_d"], chan="te0")
                S.op("sp", lambda e, p0=p0: e.dma_start(
                    out=te_d[p0:p0 + 16, :, 1, :].rearrange("pr p t -> p pr t"), in_=Es[:]),
                    reads=["Es"], writes=["te_d"], chan="te1")
                dve(lambda e, p0=p0: e.tensor_copy(out=clT[:, p0:p0 + 16], in_=Ec[:, :, 511]), ["Ec"], ["clT"])
                dve(lambda e, p0=p0: e.tensor_copy(out=slT[:, p0:p0 + 16], in_=Es[:, :, 511]), ["Es"], ["slT"])
                dve(lambda e, p0=p0: e.tensor_copy(out=rpw[:], in_=r_t[:, p0:p0 + 16]), ["r_t"], ["rpw"])
                dve(lambda e: e.memset(Rp[:, :, 511:512], 1.0), [], ["Rp"])
                n = 1
                while n < 512:
                    tt(Rp[:, :, 512 - 2 * n:512 - n], Rp[:, :, 512 - n:512],
                       rpw[:].unsqueeze(2).to_broadcast([128, 16, n]), ALU.mult, "Rp", "Rp", "rpw")
                    tt(rpw[:], rpw[:], rpw[:], ALU.mult, "rpw", "rpw", "rpw")
                    n *= 2
                dve(lambda e, p0=p0: e.tensor_copy(out=rL[:, p0:p0 + 16], in_=rpw[:]), ["rpw"], ["rL"])
                tt(Ec[:], Ec[:], Rp[:], ALU.mult, "Ec", "Ec", "Rp")
                tt(Es[:], Es[:], Rp[:], ALU.mult, "Es", "Es", "Rp")
                S.op("sp", lambda e, p0=p0: e.dma_start(
                    out=tw_d[p0:p0 + 16, :, 0, :].rearrange("pr p t -> p pr t"), in_=Ec[:]),
                    reads=["Ec"], writes=["tw_d"], chan="te0")
                S.op("sp", lambda e, p0=p0: e.dma_start(
                    out=tw_d[p0:p0 + 16, :, 1, :].rearrange("pr p t -> p pr t"), in_=Es[:]),
                    reads=["Es"], writes=["tw_d"], chan="te1")
            S.drain_key("sp", "Ec")
            S.drain_key("sp", "Es")
            S.emit()
        cast_w(w_glu, wglu_b, 1024, 4096, "wglu_b")
        cast_w(w_sc, wsc_b, 1024, 2048, "wsc_b")
        cast_w(w_o, wo_b, 2048, 2048, "wo_b")

        xt = sb("xt", [128, 4, D], F32)
        hb = sb("hb", [128, 2, D], BF16)
        hT = sb("hT", [128, KC, 512], BF16)
        wring = sb("wring", [128, 5, 2048], BF16)
        tabs = sb("tabs", [128, 2, 2, 2, 512], F32)
        ystage = hb[:].rearrange("p a d -> p (a d)").bitcast(F32)
        U = sb("U", [128, 23584], F32)
        ps = es.enter_context(nc.psum_tensor("ps", [128, 8, 512], F32))

        def carve(off_f32, nelem, dt):
            if dt == F32:
                return U[:, off_f32:off_f32 + nelem], off_f32 + nelem
            return U[:, off_f32:off_f32 + nelem // 2].bitcast(BF16), off_f32 + nelem // 2

        o = 0
        u_v, o = carve(o, 8 * 512, BF16)
        scin_v, o = carve(o, 8 * 512, BF16)
        ya_v, o = carve(o, 8 * 512, BF16)
        mg_v, o = carve(o, 16 * 512, BF16)
        gin_v, o = carve(o, 4 * 512, F32)
        G_v, o = carve(o, 4 * 512, F32)
        tmp_v, o = carve(o, 8 * 512, F32)
        hbf_v, o = carve(o, 4 * 512, BF16)
        sig_v, o = carve(o, 6 * 512, BF16)
        xsc_v, o = carve(o, 520, F32)
        ctmp_v, o = carve(o, 2 * 512, F32)
        accs_v, o = carve(o, 2 * 512, F32)
        assert o <= 23584, o
        accs_v = accs_v.rearrange("p (c t) -> p c t", c=2)
        ctmp_v = ctmp_v.rearrange("p (c t) -> p c t", c=2)
        u_v = u_v.rearrange("p (c t) -> p c t", c=8)
        scin_v = scin_v.rearrange("p (c t) -> p c t", c=8)
        ya_v = ya_v.rearrange("p (c t) -> p c t", c=8)
        mg_v = mg_v.rearrange("p (c t) -> p c t", c=16)
        gin_v = gin_v.rearrange("p (c t) -> p c t", c=4)
        G_v = G_v.rearrange("p (c t) -> p c t", c=4)
        tmp_v = tmp_v.rearrange("p (c t) -> p c t", c=8)
        hbf_v = hbf_v.rearrange("p (c t) -> p c t", c=4)
        sig_v = sig_v.rearrange("p (c t) -> p c t", c=6)
        u_alt = U[:, 2048:4096].bitcast(BF16).rearrange("p (c t) -> p c t", c=8)
        hT_alt = U[:, 6144:10240].bitcast(BF16).rearrange("p (c t) -> p c t", c=16)
        o = 0
        act_v, o = carve(o, 43 * 512, BF16)
        xff_v, o = carve(o, 4 * 520, F32)
        acf_v, o = carve(o, 4 * 512, F32)
        assert o <= 23584, o
        act_v = act_v.rearrange("p (c t) -> p c t", c=43)
        xff_v = xff_v.rearrange("p (c t) -> p c t", c=4)
        acf_v = acf_v.rearrange("p (c t) -> p c t", c=4)

        cin = [U[:, 10240 + 2048 * k:10240 + 2048 * (k + 1)] for k in range(2)]
        cout = [U[:, 4096 + 1024 * k:4096 + 1024 * (k + 1)].bitcast(BF16) for k in range(2)]
        chunks = []
        for rc in range(16):
            for a_ in range(8):
                chunks.append(("wup_b", w_up, wup_b, 128 * rc, 1376 * a_, 1376))
        for rc in range(43):
            chunks.append(("wdn_b", w_dn, wdn_b, 128 * rc, 0, 2048))
        WK["wup_b"] = []
        WK["wdn_b"] = []

        def cload_(i):
            nm, src, dst, r0, c0, nc_ = chunks[i]
            k = i % 2
            S.op("pool", lambda e: e.dma_start(out=cin[k][:, 0:nc_], in_=src[r0:r0 + 128, c0:c0 + nc_]),
                 writes=["cin%d" % k], chan="cin%d" % k)

        cload_(0)
        for i, (nm, src, dst, r0, c0, nc_) in enumerate(chunks):
            k = i % 2
            if i + 1 < len(chunks):
                cload_(i + 1)
            S.op("pool", lambda e, k=k, nc_=nc_: e.tensor_copy(out=cout[k][:, 0:nc_], in_=cin[k][:, 0:nc_]),
                 reads=["cin%d" % k], writes=["cout%d" % k])
            pk = "%s_p%d" % (nm, i)
            S.op("pool", lambda e, k=k, nc_=nc_, dst=dst, r0=r0, c0=c0: e.dma_start(
                out=dst[r0:r0 + 128, c0:c0 + nc_], in_=cout[k][:, 0:nc_]),
                reads=["cout%d" % k], writes=[pk], chan="cout%d" % k)
            WK[nm].append(pk)

        st8 = dict(bank=0, ring=0, tab=0, hb=0, ff=0, b4=0, yb=6, s5mode=False, b45=0)

        def bank():
            if st8["s5mode"]:
                b = 4 + st8["b45"]
                st8["b45"] ^= 1
                return b
            b = st8["bank"]
            st8["bank"] = (b + 1) % 6
            return b

        def bank2():
            b = st8["bank"]
            if b % 2:
                b = (b + 1) % 6
            st8["bank"] = (b + 2) % 6
            return b

        def bank4():
            b = st8["b4"]
            st8["b4"] = 4 - b
            st8["bank"] = 0
            return b

        def wload(src3, a, b, key_src):
            slot = st8["ring"]
            st8["ring"] = (slot + 1) % 4
            dst = wring[:, slot, 0:a * b].rearrange("p (a b) -> p a b", a=a)
            S.op("sp", lambda e: e.dma_start(out=dst, in_=src3), reads=WK.get(key_src, [key_src]),
                 writes=["w%d" % slot], chan="w%d" % slot)
            return dst, "w%d" % slot

        def bcload(src3, a, b):
            dst = wring[:, 4, 0:a * b].rearrange("p (a b) -> p a b", a=a)
            S.op("sp", lambda e: e.dma_start(out=dst, in_=src3), reads=WK["bc_b"], writes=["w4"], chan="w4")
            return dst, "w4"

        def ftile(wb, key, r0, nk, c0, ncol):
            src = wb[r0:r0 + 128 * nk, c0:c0 + ncol].rearrange("(k p) c -> p k c", p=128)
            return wload(src, nk, ncol, key)

        def pe(fn, reads, writes, ms):
            S.op("pe", fn, reads=reads, writes=writes, ms=ms)

        def dve(fn, reads, writes):
            S.op("dve", fn, reads=reads, writes=writes)

        def actop(fn, reads, writes):
            S.op("act", fn, reads=reads, writes=writes)

        def psk(b):
            return "ps%d" % b

        def norm_T(tts, gainT, gkey, hTd, hTk):
            for ti, (c0, R) in enumerate(tts):
                hslot = st8["hb"]
                st8["hb"] = 1 - hslot
                hk = "hb%d" % hslot
                xk = "x%d" % ti
                actop(lambda e, ti=ti, R=R, hslot=hslot: e.activation(
                    out=hb[0:R, hslot, :], in_=xt[0:R, ti, :], func=AF.Square, accum_out=ss[0:R, ti:ti + 1]),
                    [xk], [hk, "ss%d" % ti])
                actop(lambda e, ti=ti, R=R: e.activation(
                    out=rstd[0:R, ti:ti + 1], in_=ss[0:R, ti:ti + 1], func=AF.Sqrt, scale=1.0 / D, bias=EPS),
                    ["ss%d" % ti], ["rstd%d" % ti])
                dve(lambda e, ti=ti, R=R: e.reciprocal(out=rstd[0:R, ti:ti + 1], in_=rstd[0:R, ti:ti + 1]),
                    ["rstd%d" % ti], ["rstd%d" % ti])
                actop(lambda e, ti=ti, R=R, hslot=hslot: e.activation(
                    out=hb[0:R, hslot, :], in_=xt[0:R, ti, :], func=AF.Copy, scale=rstd[0:R, ti:ti + 1]),
                    [xk, "rstd%d" % ti], [hk])
                b = bank2()
                pT = ps[:, b:b + 2, :].rearrange("p a b -> p (a b)").bitcast(BF16).rearrange(
                    "p (k t) -> p k t", k=16)
                for kc in range(KC):
                    pe(lambda e, kc=kc, R=R, hslot=hslot, pT=pT: e.transpose(
                        out=pT[:, kc, 0:R], in_=hb[0:R, hslot, kc * 128:(kc + 1) * 128], identity=ident[0:R, 0:R]),
                        [hk, "ident"], [psk(b), psk(b + 1)], ms=(kc == KC - 1))
                dve(lambda e, c0=c0, R=R, pT=pT: e.tensor_tensor(
                    out=hTd[:, :, c0:c0 + R], in0=pT[:, :, 0:R],
                    in1=gainT[:].unsqueeze(2).to_broadcast([128, 16, R]), op=ALU.mult),
                    [psk(b), psk(b + 1), gkey], [hTk])

        def proj_pair(c0col, n, hTs, hTk):
            b0, b1 = bank(), bank()
            bs = (b0, b1)
            for h in range(2):
                wt, wk = ftile(win_b, "win_u" if c0col < 1024 else "win_b", h * 1024, 8, c0col, 256)
                for ci in range(2):
                    for k in range(8):
                        kc = h * 8 + k
                        pe(lambda e, wt=wt, ci=ci, k=k, kc=kc: e.matmul(
                            ps[:, bs[ci], 0:n], lhsT=wt[:, k, ci * 128:(ci + 1) * 128], rhs=hTs[:, kc, 0:n],
                            start=(kc == 0), stop=(kc == KC - 1)),
                            [wk, hTk], [psk(bs[ci])], ms=(k == 7))
            return bs

        def s5_phase(n, nseg, L, SIr_, SIi_, SOr_, SOi_, sik, sok, state_only, uS, uK, extras=()):
            yb = None

            def tt(out, a, bb, op, ko, ka, kb, eng="dve"):
                S.op(eng, lambda e: e.tensor_tensor(out=out, in0=a, in1=bb, op=op), reads=ka + kb, writes=ko)

            def tload(src, pr):
                k = st8["tab"]
                st8["tab"] = (k + 1) % 4
                tk = "tab%d" % k
                S.op("sp", lambda e: e.dma_start(out=tabs[:, k // 2, k % 2, :, :], in_=src[pr]),
                     reads=["te_d" if src is te_d else "tw_d"], writes=[tk], chan=tk)
                return k, tk

            if state_only:
                for cc in range(NCC):
                    bct, bck = bcload(bc_b[cc, :, :, 0:2, :], 8, 128)
                    for q in range(4):
                        pr = 4 * cc + q
                        k, tk = tload(tw_d, pr)
                        b0, b1 = bank(), bank()
                        pe(lambda e, bct=bct, q=q, cc=cc, b0=b0: e.matmul(
                            ps[:, b0, 0:n], lhsT=bct[:, 2 * q, :], rhs=uS[:, cc, 0:n], start=True, stop=True),
                            [bck, uK % cc], [psk(b0)], ms=False)
                        pe(lambda e, bct=bct, q=q, cc=cc, b1=b1: e.matmul(
                            ps[:, b1, 0:n], lhsT=bct[:, 2 * q + 1, :], rhs=uS[:, cc, 0:n], start=True, stop=True),
                            [bck, uK % cc], [psk(b1)], ms=True)
                        for kk_, (bb_, w) in enumerate(((b0, 0), (b1, 1), (b1, 0), (b0, 1))):
                            dve(lambda e, kk_=kk_, bb_=bb_, w=w, pr=pr, k=k: e.scalar_tensor_tensor(
                                out=tmp_v[:, kk_ % 2, 0:n], in0=ps[:, bb_, 0:n], scalar=1.0,
                                in1=tabs[:, k // 2, k % 2, w, 0:n],
                                op0=ALU.mult, op1=ALU.mult, accum_out=accp[:, pr, kk_:kk_ + 1]),
                                [psk(bb_), tk], ["tmp%d" % (kk_ % 2), "accp"])
                A = lambda k: accp[:, :, k]
                tt(gtr[:], Hr[:], rL[:], ALU.mult, ["gtr"], ["H"], ["rL"])
                tt(gtr[:], gtr[:], A(0), ALU.add, ["gtr"], ["gtr"], ["accp"])
                tt(gtr[:], gtr[:], A(1), ALU.add, ["gtr"], ["gtr"], ["accp"])
                tt(gti[:], Hi[:], rL[:], ALU.mult, ["gti"], ["H"], ["rL"])
                tt(gti[:], gti[:], A(2), ALU.add, ["gti"], ["gti"], ["accp"])
                tt(gti[:], gti[:], A(3), ALU.subtract, ["gti"], ["gti"], ["accp"])
                tt(gtt[:], gti[:], slT[:], ALU.mult, ["gtt"], ["gti"], ["slT"])
                tt(Hr[:], gtr[:], clT[:], ALU.mult, ["H"], ["gtr"], ["clT"])
                tt(Hr[:], Hr[:], gtt[:], ALU.subtract, ["H"], ["H"], ["gtt"])
                tt(gtt[:], gti[:], clT[:], ALU.mult, ["gtt"], ["gti"], ["clT"])
                tt(Hi[:], gtr[:], slT[:], ALU.mult, ["H"], ["gtr"], ["slT"])
                tt(Hi[:], Hi[:], gtt[:], ALU.add, ["H"], ["H"], ["gtt"])
                return

            def v4(ap3):
                return ap3.rearrange("p j (s l) -> p j s l", s=nseg)

            for cc in range(NCC):
                bct, bck = bcload(bc_b[cc].rearrange("p q w n -> p (q w) n"), 16, 128)
                for pp in range(2):
                    pr0 = 4 * cc + 2 * pp
                    ks = [tload(te_d, pr0 + j) for j in range(2)]
                    assert ks[0][0] % 2 == 0
                    tb = ks[0][0] // 2
                    tks = [ks[0][1], ks[1][1]]
                    for j in range(2):
                        q = 2 * pp + j
                        pe(lambda e, bct=bct, q=q, cc=cc, j=j: e.matmul(
                            ps[:, j, 0:n], lhsT=bct[:, 4 * q + 0, :], rhs=uS[:, cc, 0:n], start=True, stop=True),
                            [bck, uK % cc], [psk(j)], ms=False)
                        pe(lambda e, bct=bct, q=q, cc=cc, j=j: e.matmul(
                            ps[:, 2 + j, 0:n], lhsT=bct[:, 4 * q + 1, :], rhs=uS[:, cc, 0:n], start=True, stop=True),
                            [bck, uK % cc], [psk(2 + j)], ms=(j == 1))
                    if nseg == 1:
                        bcT = lambda w: tabs[:, tb, :, w, 0:L]
                        vv = lambda ap3: ap3
                    else:
                        bcT = lambda w: tabs[:, tb, :, w, 0:L].unsqueeze(2).to_broadcast([128, 2, nseg, L])
                        vv = v4
                    cT, sT = bcT(0), bcT(1)
                    PrB = vv(ps[:, 0:2, 0:n])
                    PiB = vv(ps[:, 2:4, 0:n])
                    T = [vv(tmp_v[:, 2 * i:2 * i + 2, 0:n]) for i in range(4)]
                    TK = [["tmp%d" % i] for i in range(4)]
                    gr = vv(gin_v[:, 0:2, 0:n])
                    gi = vv(gin_v[:, 2:4, 0:n])
                    kPr = [psk(0), psk(1)]
                    kPi = [psk(2), psk(3)]
                    tt(T[0], PrB, cT, ALU.mult, TK[0], kPr, tks)
                    tt(T[1], PiB, sT, ALU.mult, TK[1], kPi, tks)
                    tt(gr, T[0], T[1], ALU.add, ["gin0"], TK[0], TK[1], eng="pool")
                    tt(T[2], PiB, cT, ALU.mult, TK[2], kPi, tks)
                    tt(T[3], PrB, sT, ALU.mult, TK[3], kPr, tks)
                    tt(gi, T[2], T[3], ALU.subtract, ["gin1"], TK[2], TK[3], eng="pool")
                    for w, (SI_, gk) in enumerate(((SIr_, "gin0"), (SIi_, "gin1"))):
                        for j in range(2):
                            for sg in range(nseg):
                                dve(lambda e, sg=sg, w=w, j=j, SI_=SI_, pr=pr0 + j: e.tensor_tensor_scan(
                                    out=G_v[:, 2 * w + j, sg * L:(sg + 1) * L],
                                    data0=r_t[:, pr:pr + 1].to_broadcast([128, L]),
                                    data1=gin_v[:, 2 * w + j, sg * L:(sg + 1) * L],
                                    initial=SI_[:, sg, pr:pr + 1], op0=ALU.mult, op1=ALU.add),
                                    [gk, "r_t", sik], ["G%d" % w])
                    GrL = v4(G_v[:, 0:2, 0:n])[:, :, :, L - 1]
                    GiL = v4(G_v[:, 2:4, 0:n])[:, :, :, L - 1]
                    clB = tabs[:, tb, :, 0, L - 1:L].to_broadcast([128, 2, nseg])
                    slB = tabs[:, tb, :, 1, L - 1:L].to_broadcast([128, 2, nseg])
                    tA = tiny[:, 0, 0:2 * nseg].rearrange("p (j s) -> p j s", j=2)
                    tB = tiny[:, 1, 0:2 * nseg].rearrange("p (j s) -> p j s", j=2)
                    sor = SOr_[:, :, pr0:pr0 + 2].rearrange("p s q -> p q s")
                    soi = SOi_[:, :, pr0:pr0 + 2].rearrange("p s q -> p q s")
                    tt(tA, GiL, slB, ALU.mult, ["tinyA"], ["G1"], tks)
                    tt(tB, GrL, clB, ALU.mult, ["tinyB"], ["G0"], tks)
                    tt(sor, tB, tA, ALU.subtract, [sok], ["tinyB"], ["tinyA"])
                    tt(tA, GiL, clB, ALU.mult, ["tinyA"], ["G1"], tks)
                    tt(tB, GrL, slB, ALU.mult, ["tinyB"], ["G0"], tks)
                    tt(soi, tB, tA, ALU.add, [sok], ["tinyB"], ["tinyA"])
                    GrB = vv(G_v[:, 0:2, 0:n])
                    GiB = vv(G_v[:, 2:4, 0:n])
                    hr = vv(hbf_v[:, 0:2, 0:n])
                    hi = vv(hbf_v[:, 2:4, 0:n])
                    tt(T[0], GrB, cT, ALU.mult, TK[0], ["G0"], tks)
                    tt(T[1], GiB, sT, ALU.mult, TK[1], ["G1"], tks)
                    tt(hr, T[0], T[1], ALU.subtract, ["hbfr"], TK[0], TK[1], eng="pool")
                    tt(T[2], GrB, sT, ALU.mult, TK[2], ["G0"], tks)
                    tt(T[3], GiB, cT, ALU.mult, TK[3], ["G1"], tks)
                    tt(hi, T[2], T[3], ALU.add, ["hbfi"], TK[2], TK[3], eng="pool")
                    ppi = 2 * cc + pp
                    xi = {0: 0, 2: 1, 4: 2, 6: 3}.get(ppi, ppi - 4 if ppi >= 8 else None)
                    if extras and xi is not None and xi < len(extras):
                        st8["s5mode"] = True
                        extras[xi]()
                        st8["s5mode"] = False
                    if pp == 0:
                        yb = st8["yb"]
                        st8["yb"] = 13 - yb
                    for j in range(2):
                        q = 2 * pp + j
                        pe(lambda e, bct=bct, q=q, j=j, yb=yb: e.matmul(
                            ps[:, yb, 0:n], lhsT=bct[:, 4 * q + 2, :], rhs=hbf_v[:, j, 0:n], start=(q == 0),
                            stop=False), [bck, "hbfr"], [psk(yb)], ms=False)
                        pe(lambda e, bct=bct, q=q, j=j, yb=yb: e.matmul(
                            ps[:, yb, 0:n], lhsT=bct[:, 4 * q + 3, :], rhs=hbf_v[:, 2 + j, 0:n], start=False,
                            stop=(q == 3)), [bck, "hbfi"], [psk(yb)], ms=(j == 1))
                dve(lambda e, cc=cc, yb=yb: e.scalar_tensor_tensor(
                    out=tmp_v[:, 0, 0:n], in0=uS[:, cc, 0:n], scalar=dsk[:, cc:cc + 1], in1=ps[:, yb, 0:n],
                    op0=ALU.mult, op1=ALU.add), [uK % cc, "dsk", psk(yb)], ["tmp0"])
                actop(lambda e, cc=cc: e.activation(out=ya_v[:, cc, 0:n], in_=tmp_v[:, 0, 0:n],
                                                    func=AF.Gelu_apprx_tanh), ["tmp0"], ["ya"])
            st8["bank"] = 0

        def conv3(xbuf3, acc3, wts, bias, nseg, L, kx, ka, kw, kb):
            actop(lambda e: e.activation(out=acc3, in_=xbuf3[:, :, 2:L + 2], func=AF.Identity,
                                         scale=wts[:, 2:3], bias=bias), [kx, kw, kb], [ka])
            dve(lambda e: e.scalar_tensor_tensor(out=acc3, in0=xbuf3[:, :, 1:L + 1], scalar=wts[:, 1:2], in1=acc3,
                                                  op0=ALU.mult, op1=ALU.add), [kx, kw, ka], [ka])
            dve(lambda e: e.scalar_tensor_tensor(out=acc3, in0=xbuf3[:, :, 0:L], scalar=wts[:, 0:1], in1=acc3,
                                                  op0=ALU.mult, op1=ALU.add), [kx, kw, ka], [ka])

        def block(xsrc, row0, n, nseg, L, mode, ydst=None, SI=None, SO=None, sik="H", sok="H", alt=0):
            hTd, hTk = (hT_alt, "hTb") if alt else (hT, "hT")
            uS, uK = (u_alt, "ub%d") if alt else (u_v, "u%d")
            tts = []
            c = 0
            while c < n:
                R = min(128, n - c)
                tts.append((c, R))
                c += R
            v3 = lambda ap: ap.rearrange("p (s l) -> p s l", s=nseg)
            for ti, (c0, R) in enumerate(tts):
                S.op("act", lambda e, ti=ti, c0=c0, R=R: e.dma_start(
                    out=xt[0:R, ti, :], in_=xsrc[row0 + c0:row0 + c0 + R, :]), writes=["x%d" % ti],
                    chan="x%d" % ti)
            norm_T(tts, n1g, "n1g", hTd, hTk)
            state_only = (mode == "pre")
            for j in range(4):
                bu_ = proj_pair(256 * j, n, hTd, hTk)
                for ci in range(2):
                    cc = 2 * j + ci
                    actop(lambda e, cc=cc, b=bu_[ci]: e.activation(out=uS[:, cc, 0:n], in_=ps[:, b, 0:n],
                                                                    func=AF.Copy), [psk(bu_[ci])], [uK % cc])

            def grp(wb, key, K, c0col, rhs3, rkey):
                bs = (bank(), bank())
                nh = K // 1024
                for h in range(nh):
                    wt, wk = ftile(wb, key, h * 1024, 8, c0col, 256)
                    for ci in range(2):
                        for k in range(8):
                            kc = h * 8 + k
                            pe(lambda e, wt=wt, ci=ci, k=k, kc=kc, bs=bs: e.matmul(
                                ps[:, bs[ci], 0:n], lhsT=wt[:, k, ci * 128:(ci + 1) * 128], rhs=rhs3[:, kc, 0:n],
                                start=(kc == 0), stop=(kc == K // 128 - 1)),
                                [wk, rkey], [psk(bs[ci])], ms=(k == 7))
                return bs

            def mk2b(j):
                def f():
                    bc_ = proj_pair(2048 + 256 * j, n, hTd, hTk)
                    for ci in range(2):
                        actop(lambda e, ci=ci, b=bc_[ci]: e.activation(out=ctmp_v[:, ci, 0:n], in_=ps[:, b, 0:n],
                                                                        func=AF.Copy), [psk(bc_[ci])], ["ctmp%d" % ci])
                    bv_ = proj_pair(3072 + 256 * j, n, hTd, hTk)
                    for ci in range(2):
                        cc = 2 * j + ci
                        xb3 = xsc_v[:, 0:nseg * (L + 2)].rearrange("p (s l) -> p s l", s=nseg)
                        dve(lambda e, cc=cc, xb3=xb3: e.tensor_copy(out=xb3[:, :, 0:2], in_=hs_sc[:, cc, 0:nseg, :]),
                            ["hs_sc"], ["xsc"])
                        dve(lambda e, ci=ci, b=bv_[ci], xb3=xb3: e.tensor_tensor(
                            out=xb3[:, :, 2:L + 2], in0=v3(ctmp_v[:, ci, 0:n]), in1=v3(ps[:, b, 0:n]), op=ALU.mult),
                            ["ctmp%d" % ci, psk(bv_[ci])], ["xsc"])
                        dve(lambda e, cc=cc, xb3=xb3: e.tensor_copy(out=hs_sc[:, cc, 0:nseg, :],
                                                                    in_=xb3[:, :, L:L + 2]),
                            ["xsc"], ["hs_sc"])
                        acc3 = v3(accs_v[:, ci, 0:n])
                        conv3(xb3, acc3, scw[:, cc, :], scb[:, cc:cc + 1], nseg, L, "xsc", "accs%d" % ci, "scw", "scb")
                    bg_ = proj_pair(1024 + 256 * j, n, hTd, hTk)
                    for ci in range(2):
                        cc = 2 * j + ci
                        dve(lambda e, cc=cc, ci=ci, b=bg_[ci]: e.tensor_tensor(
                            out=scin_v[:, cc, 0:n], in0=accs_v[:, ci, 0:n], in1=ps[:, b, 0:n], op=ALU.mult),
                            ["accs%d" % ci, psk(bg_[ci])], ["scin"])
                return f

            def mk4(jj):
                def f():
                    bgb = grp(win_b, "win_b", 2048, 6144 + 256 * jj, hT, "hT")
                    for ci in range(2):
                        actop(lambda e, ci=ci, b=bgb[ci]: e.activation(out=sig_v[:, 4 + ci, 0:n], in_=ps[:, b, 0:n],
                                                                        func=AF.Sigmoid), [psk(bgb[ci])],
                              ["sig%d" % (4 + ci)])
                    bso = grp(wsc_b, "wsc_b", 1024, 256 * jj, scin_v, "scin")
                    for ci in range(2):
                        jm = 2 * jj + ci
                        dve(lambda e, ci=ci, b=bso[ci], jm=jm: e.tensor_tensor(
                            out=mg_v[:, jm, 0:n], in0=ps[:, b, 0:n], in1=sig_v[:, 4 + ci, 0:n], op=ALU.mult),
                            [psk(bso[ci]), "sig%d" % (4 + ci)], ["mg%d" % jm])
                return f

            extras = [] if state_only else [mk2b(j) for j in range(4)] + [mk4(jj) for jj in range(8)]
            s5_phase(n, nseg, L, SI[0], SI[1], SO[0], SO[1], sik, sok, state_only, uS, uK, extras)
            if state_only:
                return
            for jj in range(8):
                bgh = grp(wglu_b, "wglu_b", 1024, 2048 + 256 * jj, ya_v, "ya")
                for ci in range(2):
                    actop(lambda e, ci=ci, b=bgh[ci]: e.activation(out=sig_v[:, ci, 0:n], in_=ps[:, b, 0:n],
                                                                    func=AF.Sigmoid), [psk(bgh[ci])], ["sig%d" % ci])
                bga = grp(win_b, "win_b", 2048, 4096 + 256 * jj, hT, "hT")
                for ci in range(2):
                    actop(lambda e, ci=ci, b=bga[ci]: e.activation(out=sig_v[:, 2 + ci, 0:n], in_=ps[:, b, 0:n],
                                                                    func=AF.Sigmoid), [psk(bga[ci])],
                          ["sig%d" % (2 + ci)])
                bgl = grp(wglu_b, "wglu_b", 1024, 256 * jj, ya_v, "ya")
                for ci in range(2):
                    jm = 2 * jj + ci
                    m1 = tmp_v[:, ci, 0:n]
                    dve(lambda e, ci=ci, b=bgl[ci], m1=m1: e.tensor_tensor(
                        out=m1, in0=ps[:, b, 0:n], in1=sig_v[:, ci, 0:n], op=ALU.mult),
                        [psk(bgl[ci]), "sig%d" % ci], ["tmp%d" % ci])
                    dve(lambda e, ci=ci, m1=m1: e.tensor_tensor(
                        out=m1, in0=m1, in1=sig_v[:, 2 + ci, 0:n], op=ALU.mult), ["tmp%d" % ci, "sig%d" % (2 + ci)],
                        ["tmp%d" % ci])
                    dve(lambda e, jm=jm, m1=m1: e.tensor_tensor(
                        out=mg_v[:, jm, 0:n], in0=mg_v[:, jm, 0:n], in1=m1, op=ALU.add),
                        ["tmp%d" % ci, "mg%d" % jm], ["mg%d" % jm])

            def tok_proj(wb, key, nk, lhs3, lkey):
                for cb in range(4):
                    b4 = bank4()
                    k0 = 0
                    while k0 < nk:
                        kn = min(4, nk - k0)
                        src = wb[128 * k0:128 * (k0 + kn), cb * 512:(cb + 1) * 512].rearrange(
                            "(k p) c -> p k c", p=128)
                        wt, wk = wload(src, kn, 512, key)
                        for kl in range(kn):
                            kc = k0 + kl
                            for ti, (c0, R) in enumerate(tts):
                                pe(lambda e, wt=wt, kl=kl, kc=kc, ti=ti, c0=c0, R=R, b4=b4: e.matmul(
                                    ps[0:R, b4 + ti, :], lhsT=lhs3[:, kc, c0:c0 + R], rhs=wt[:, kl, :],
                                    start=(kc == 0), stop=(kc == nk - 1)),
                                    [wk, (lkey if lkey else "mg%d" % kc)], [psk(b4 + ti)],
                                    ms=(kl == kn - 1 and ti == len(tts) - 1))
                        k0 += kn
                    for ti, (c0, R) in enumerate(tts):
                        dve(lambda e, ti=ti, R=R, cb=cb, b4=b4: e.tensor_tensor(
                            out=xt[0:R, ti, cb * 512:(cb + 1) * 512], in0=xt[0:R, ti, cb * 512:(cb + 1) * 512],
                            in1=ps[0:R, b4 + ti, :], op=ALU.add), ["x%d" % ti, psk(b4 + ti)], ["x%d" % ti])

            tok_proj(wo_b, "wo_b", 16, mg_v, None)
            norm_T(tts, n2g, "n2g", hT, "hT")
            for ip in range(22):
                ncol = 256 if ip < 21 else 128
                nci = ncol // 128
                bvs = [bank() for _ in range(nci)]
                bgs = [bank() for _ in range(nci)]
                for (c0col, bsx) in ((256 * ip, bvs), (5504 + 256 * ip, bgs)):
                    for h in range(2):
                        wt, wk = ftile(wup_b, "wup_b", h * 1024, 8, c0col, ncol)
                        for ci in range(nci):
                            for k in range(8):
                                kc = h * 8 + k
                                pe(lambda e, wt=wt, ci=ci, k=k, kc=kc, bsx=bsx: e.matmul(
                                    ps[:, bsx[ci], 0:n], lhsT=wt[:, k, ci * 128:(ci + 1) * 128], rhs=hT[:, kc, 0:n],
                                    start=(kc == 0), stop=(kc == KC - 1)),
                                    [wk, "hT"], [psk(bsx[ci])], ms=(k == 7))
                for ci in range(nci):
                    i = 2 * ip + ci
                    fs = st8["ff"]
                    st8["ff"] = 1 - fs
                    for w, (bsx, ch) in enumerate(((bvs, i), (bgs, 43 + i))):
                        xk = "xff%d" % (2 * fs + w)
                        ak = "acf%d" % (2 * fs + w)
                        xb3 = xff_v[:, 2 * fs + w, 0:nseg * (L + 2)].rearrange("p (s l) -> p s l", s=nseg)
                        acc3 = v3(acf_v[:, 2 * fs + w, 0:n])
                        dve(lambda e, ch=ch, xb3=xb3: e.tensor_copy(out=xb3[:, :, 0:2], in_=hs_ff[:, ch, 0:nseg, :]),
                            ["hs_ff"], [xk])
                        actop(lambda e, b=bsx[ci], xb3=xb3: e.activation(out=xb3[:, :, 2:L + 2], in_=v3(ps[:, b, 0:n]),
                                                                          func=AF.Copy), [psk(bsx[ci])], [xk])
                        dve(lambda e, ch=ch, xb3=xb3: e.tensor_copy(out=hs_ff[:, ch, 0:nseg, :], in_=xb3[:, :, L:L + 2]),
                            [xk], ["hs_ff"])
                        conv3(xb3, acc3, ffw[:, ch, :], ffb[:, ch:ch + 1], nseg, L, xk, ak, "ffw", "ffb")
                    ag = acf_v[:, 2 * fs + 1, 0:n]
                    av = acf_v[:, 2 * fs + 0, 0:n]
                    actop(lambda e, ag=ag: e.activation(out=ag, in_=ag, func=AF.Silu),
                          ["acf%d" % (2 * fs + 1)], ["acf%d" % (2 * fs + 1)])
                    dve(lambda e, i=i, ag=ag, av=av: e.tensor_tensor(out=act_v[:, i, 0:n], in0=ag, in1=av, op=ALU.mult),
                        ["acf%d" % (2 * fs + 1), "acf%d" % (2 * fs)], ["actv"])
            tok_proj(wdn_b, "wdn_b", 43, act_v, "actv")
            for ti, (c0, R) in enumerate(tts):
                xk = "x%d" % ti
                bj = bank4()
                pj = ps[:, bj:bj + 4, :].rearrange("p a b -> p (a b)")
                actop(lambda e, ti=ti, R=R, pj=pj: e.activation(
                    out=pj[0:R, :], in_=xt[0:R, ti, :], func=AF.Square, accum_out=ss[0:R, ti:ti + 1]),
                    [xk], [psk(bj), psk(bj + 1), psk(bj + 2), psk(bj + 3), "ss%d" % ti])
                actop(lambda e, ti=ti, R=R: e.activation(
                    out=rstd[0:R, ti:ti + 1], in_=ss[0:R, ti:ti + 1], func=AF.Sqrt, scale=1.0 / D, bias=EPS),
                    ["ss%d" % ti], ["rstd%d" % ti])
                dve(lambda e, ti=ti, R=R: e.reciprocal(out=rstd[0:R, ti:ti + 1], in_=rstd[0:R, ti:ti + 1]),
                    ["rstd%d" % ti], ["rstd%d" % ti])
                dve(lambda e, ti=ti, R=R: e.scalar_tensor_tensor(
                    out=ystage[0:R, :], in0=xt[0:R, ti, :], scalar=rstd[0:R, ti:ti + 1], in1=fg[0:R, :],
                    op0=ALU.mult, op1=ALU.mult), [xk, "rstd%d" % ti, "fg"], ["hb0", "hb1"])
                S.op("act", lambda e, c0=c0, R=R: e.dma_start(out=ydst[row0 + c0:row0 + c0 + R, :],
                                                                in_=ystage[0:R, :]),
                     reads=["hb0", "hb1"], chan="yout")

        Hv = (Hr[:].unsqueeze(1), Hi[:].unsqueeze(1))
        for bi_, r0 in enumerate(range(0, HALF, 512)):
            block(xprev, r0, 512, 1, 512, "pre", SI=Hv, SO=Hv, alt=bi_ % 2)
        S.drain_key("act", "cout0")
        S.drain_key("act", "cout1")
        S.drain_key("dve", "cin0")
        S.drain_key("dve", "cin1")
        dve(lambda e: e.tensor_copy(out=SIr[:, 0, :], in_=Hr[:]), ["H"], ["SI"])
        dve(lambda e: e.tensor_copy(out=SIi[:, 0, :], in_=Hi[:]), ["H"], ["SI"])
        block(xsmall, 0, 80, 5, 16, "full", ydst=y_small, SI=(SIr[:], SIi[:]), SO=(SOr[:], SOi[:]), sik="SI",
              sok="SO")
        dve(lambda e: e.tensor_copy(out=Hr[:], in_=SOr[:, 0, :]), ["SO"], ["H"])
        dve(lambda e: e.tensor_copy(out=Hi[:], in_=SOi[:, 0, :]), ["SO"], ["H"])

        sh5 = [128, 5, 32]
        fb5 = lambda t: t[:].unsqueeze(1).to_broadcast(sh5)
        xA = SIr
        xB = SIi
        dve(lambda e: e.tensor_tensor(out=xA[:], in0=SOr[:], in1=fb5(fr_t), op=ALU.mult), ["SO", "SI"], ["SI"])
        dve(lambda e: e.tensor_tensor(out=xB[:], in0=SOi[:], in1=fb5(fi_t), op=ALU.mult), ["SO", "SI"], ["SI"])
        dve(lambda e: e.tensor_tensor(out=s5st[:, 0, 0:5, :], in0=xA[:], in1=xB[:], op=ALU.subtract), ["SI"],
            ["s5st"])
        dve(lambda e: e.tensor_tensor(out=xA[:], in0=SOr[:], in1=fb5(fi_t), op=ALU.mult), ["SO", "SI"], ["SI"])
        dve(lambda e: e.tensor_tensor(out=xB[:], in0=SOi[:], in1=fb5(fr_t), op=ALU.mult), ["SO", "SI"], ["SI"])
        dve(lambda e: e.tensor_tensor(out=s5st[:, 1, 0:5, :], in0=xA[:], in1=xB[:], op=ALU.add), ["SI"], ["s5st"])
        S.op("sp", lambda e: e.dma_start(out=sco_d[:, :, 0:5, :], in_=hs_sc[:]), reads=["hs_sc"], chan="o_sc")
        S.op("sp", lambda e: e.dma_start(out=ffo_d[:, :, 0:5, :], in_=hs_ff[:]), reads=["hs_ff"], chan="o_ff")
        for bi in range(NBLK):
            block(xown, bi * 512, 512, 1, 512, "full", ydst=y_own, SI=Hv, SO=Hv)
        xA2 = SIr[:, 0, :]
        xB2 = SIi[:, 0, :]
        dve(lambda e: e.tensor_tensor(out=xA2, in0=Hr[:], in1=fr_t[:], op=ALU.mult), ["H", "SI"], ["SI"])
        dve(lambda e: e.tensor_tensor(out=xB2, in0=Hi[:], in1=fi_t[:], op=ALU.mult), ["H", "SI"], ["SI"])
        dve(lambda e: e.tensor_tensor(out=s5st[:, 0, 5, :], in0=xA2, in1=xB2, op=ALU.subtract), ["SI"], ["s5st"])
        dve(lambda e: e.tensor_tensor(out=xA2, in0=Hr[:], in1=fi_t[:], op=ALU.mult), ["H", "SI"], ["SI"])
        dve(lambda e: e.tensor_tensor(out=xB2, in0=Hi[:], in1=fr_t[:], op=ALU.mult), ["H", "SI"], ["SI"])
        dve(lambda e: e.tensor_tensor(out=s5st[:, 1, 5, :], in0=xA2, in1=xB2, op=ALU.add), ["SI"], ["s5st"])

        S.op("sp", lambda e: e.dma_start(out=s5o_d, in_=s5st[:]), reads=["s5st"], chan="o_s5")
        S.op("sp", lambda e: e.dma_start(out=sco_d[:, :, 5, :], in_=hs_sc[:, :, 0, :]), reads=["hs_sc"], chan="o_sc")
        S.op("sp", lambda e: e.dma_start(out=ffo_d[:, :, 5, :], in_=hs_ff[:, :, 0, :]), reads=["hs_ff"], chan="o_ff")
        S.final_wait("sp")
        S.emit()
        build.nops = S.nops
    return nc


_CACHE = {}


def _layout_common(inp):
    f = np.float32
    A = np.ascontiguousarray
    out = {}
    out["w_in"] = A(inp["w_in"][0], dtype=f)
    out["w_glu"] = A(inp["w_glu"][0], dtype=f)
    out["w_sc"] = A(inp["w_sc_out"][0], dtype=f)
    out["w_o"] = A(inp["w_o"][0], dtype=f)
    out["w_up"] = A(inp["w_up"][0], dtype=f)
    out["w_dn"] = A(inp["w_down"][0], dtype=f)
    bpad = np.zeros((2, 32, 128, 128), f)
    cpad = np.zeros((2, 32, 128, 128), f)
    for w, (bb, cc_) in enumerate(((inp["b_re"][0], inp["c_re"][0]), (inp["b_im"][0], inp["c_im"][0]))):
        for pr in range(32):
            for j in range(2):
                g = 2 * pr + j
                r0 = 32 * (pr % 4) + 16 * j
                bpad[w, pr, r0:r0 + 16, 64 * j:64 * j + 64] = bb[g].T
                cpad[w, pr, 64 * j:64 * j + 64, r0:r0 + 16] = cc_[g].T
    out["bpad"] = bpad
    out["cpad"] = cpad

    def pairlay(a):
        return A(a.reshape(32, 2, 64).transpose(1, 2, 0).reshape(128, 32), dtype=f)

    out["lamre"] = pairlay(inp["lam_re"][0])
    out["lamim"] = pairlay(inp["lam_im"][0])
    out["logdt"] = pairlay(np.broadcast_to(inp["log_dt"][0][:, None], (64, 64)))
    out["dskip"] = A(inp["d_skip"][0].reshape(8, 128).T, dtype=f)
    out["scw"] = A(inp["sc_conv_w"][0].reshape(3, 8, 128).transpose(2, 1, 0), dtype=f)
    out["scb"] = A(inp["sc_conv_b"][0].reshape(8, 128).T, dtype=f)
    out["ffw"] = A(inp["ffn_conv_w"][0].reshape(3, 86, 128).transpose(2, 1, 0), dtype=f)
    out["ffb"] = A(inp["ffn_conv_b"][0].reshape(86, 128).T, dtype=f)
    out["n1g"] = A(inp["norm1_g"][0].reshape(16, 128).T, dtype=f)
    out["n2g"] = A(inp["norm2_g"][0].reshape(16, 128).T, dtype=f)
    out["fg"] = A(np.broadcast_to(inp["final_norm_g"][None, :], (128, D)), dtype=f)
    return out


def kernel(**inp):
    f = np.float32
    A = np.ascontiguousarray
    xp = np.asarray(inp["x_prompt"])
    xs = np.asarray(inp["x_sample"])
    B, SEQ, _ = xp.shape
    HALF = SEQ // 2
    PRE = HALF - 16
    assert B == 4 and xs.shape[0] == 32 and xs.shape[1] == 16
    inp = {k: np.asarray(v) for k, v in inp.items()}
    common = _layout_common(inp)
    in_maps = []
    for c in range(8):
        b, half = c // 2, c % 2
        m = dict(common)
        if half == 0:
            prev = np.zeros((HALF, D), f)
        else:
            prev = xp[b, 0:HALF]
        m["xprev"] = A(np.concatenate([np.zeros((16, D), f), prev[0:PRE]], 0), dtype=f)
        m["xsmall"] = A(np.concatenate([prev[PRE:HALF], xs[4 * c:4 * c + 4].reshape(64, D)], 0), dtype=f)
        m["xown"] = A(xp[b, half * HALF:(half + 1) * HALF], dtype=f)
        sl = slice(4 * c, 4 * c + 4)
        for nm, src in (("sre", inp["state_s5_re"][0, sl]), ("sim", inp["state_s5_im"][0, sl])):
            m[nm] = A(src.reshape(4, 32, 2, 64).transpose(2, 3, 0, 1).reshape(128, 4, 32), dtype=f)
        m["csc"] = A(inp["cache_sc_conv"][0, sl].reshape(4, 2, 8, 128).transpose(3, 2, 0, 1), dtype=f)
        m["cff"] = A(inp["cache_ffn_conv"][0, sl].reshape(4, 2, 86, 128).transpose(3, 2, 0, 1), dtype=f)
        in_maps.append(m)
    if HALF not in _CACHE:
        _CACHE[HALF] = build(HALF)
    nc = _CACHE[HALF]
    res = run_bass_kernel_spmd(nc, in_maps, core_ids=list(range(8))).results

    y_prompt = np.zeros((4, SEQ, D), f)
    y_sample = np.zeros((32, 16, D), f)
    p_re = np.zeros((1, 4, 64, 64), f)
    p_im = np.zeros((1, 4, 64, 64), f)
    p_sc = np.zeros((1, 4, 2, 1024), f)
    p_ff = np.zeros((1, 4, 2, 11008), f)
    s_re = np.zeros((1, 32, 64, 64), f)
    s_im = np.zeros((1, 32, 64, 64), f)
    s_sc = np.zeros((1, 32, 2, 1024), f)
    s_ff = np.zeros((1, 32, 2, 11008), f)

    def unpair(a):
        return a.reshape(2, 64, 32).transpose(2, 0, 1).reshape(64, 64)

    def uncache(a):
        return a.transpose(2, 1, 0).reshape(2, -1)

    for c in range(8):
        b, half = c // 2, c % 2
        r = res[c]
        y_prompt[b, half * HALF:(half + 1) * HALF] = r["y_own"]
        y_sample[4 * c:4 * c + 4] = r["y_small"][16:80].reshape(4, 16, D)
        for s in range(4):
            s_re[0, 4 * c + s] = unpair(r["s5o"][:, 0, 1 + s, :])
            s_im[0, 4 * c + s] = unpair(r["s5o"][:, 1, 1 + s, :])
            s_sc[0, 4 * c + s] = uncache(r["sco"][:, :, 1 + s, :])
            s_ff[0, 4 * c + s] = uncache(r["ffo"][:, :, 1 + s, :])
        if half == 1:
            p_re[0, b] = unpair(r["s5o"][:, 0, 5, :])
            p_im[0, b] = unpair(r["s5o"][:, 1, 5, :])
            p_sc[0, b] = uncache(r["sco"][:, :, 5, :])
            p_ff[0, b] = uncache(r["ffo"][:, :, 5, :])
    return (y_prompt, y_sample, p_re, p_im, p_sc, p_ff, s_re, s_im, s_sc, s_ff)
```
